# Optimizing a Trainium2 kernel written in Bass

```python
import jax
import jax.numpy as jnp
from jax import lax
import numpy as np


D_MODEL = 1024
BATCH = 8
SEQ = 2048
DEPTH = 2

RMS_EPS = 1e-6
SSD_WIDTH = D_MODEL
SSD_HEAD_DIM = 64
SSD_HEADS = SSD_WIDTH // SSD_HEAD_DIM
SSD_GROUPS = 2
SSD_HEADS_PER_GROUP = SSD_HEADS // SSD_GROUPS
SSD_STATE = 64
SSD_CONV = 4
SSD_CHUNK = 128
SSD_CONV_DIM = SSD_WIDTH + 2 * SSD_GROUPS * SSD_STATE
GLA_HEADS = 8
GLA_VALUE_WIDTH = D_MODEL
GLA_KEY_WIDTH = D_MODEL // 2
GLA_HEAD_K = GLA_KEY_WIDTH // GLA_HEADS
GLA_HEAD_V = GLA_VALUE_WIDTH // GLA_HEADS
GLA_GATE_RANK = 16
GLA_GATE_NORMALIZER = 16.0
GLA_CHUNK = 64
EVEN_SPLITS = (SSD_WIDTH, SSD_CONV_DIM, SSD_HEADS, GLA_KEY_WIDTH, GLA_KEY_WIDTH, GLA_VALUE_WIDTH, GLA_GATE_RANK, GLA_VALUE_WIDTH)
EVEN_IN_WIDTH = sum(EVEN_SPLITS)
EVEN_MIX_WIDTH = SSD_WIDTH + GLA_VALUE_WIDTH
MOBA_HEADS = 16
MOBA_HEAD_DIM = 64
MOBA_WIDTH = MOBA_HEADS * MOBA_HEAD_DIM
MOBA_BLOCK = 256
MOBA_TOPK = 3
MOBA_Q_CHUNK = 32
ODD_IN_WIDTH = 4 * MOBA_WIDTH
N_EVEN = (DEPTH + 1) // 2
N_ODD = DEPTH // 2

kernel_name = 'hybrid_ssd_gla_moba_trunk'


def rms_norm(x, w):
    x32 = x.astype(jnp.float32)
    y = x32 * lax.rsqrt(jnp.mean(x32 * x32, axis=-1, keepdims=True) + RMS_EPS)
    return y * w.astype(jnp.float32)


def split_cols(u, sizes):
    cuts = [int(c) for c in np.cumsum(sizes)[:-1]]
    return jnp.split(u, cuts, axis=-1)


def causal_depthwise_conv(u, w, b):
    k_width = w.shape[0]
    y = lax.conv_general_dilated(u, w.astype(u.dtype)[:, None, :], window_strides=(1,), padding=[(k_width - 1, 0)], dimension_numbers=('NWC', 'WIO', 'NWC'), feature_group_count=u.shape[-1])
    return y + b.astype(u.dtype)


def ssd_chunked_scan(xs, dt, a, bm, cm, d_skip):
    b, L, G, E, P = xs.shape
    N = bm.shape[-1]
    Q = SSD_CHUNK
    nc = L // Q
    x_c = xs.reshape(b, nc, Q, G, E, P)
    dt_c = dt.reshape(b, nc, Q, G, E)
    xdt = x_c * dt_c[..., None]
    b_c = bm.reshape(b, nc, Q, G, N)
    c_c = cm.reshape(b, nc, Q, G, N)
    a_cs = jnp.cumsum(dt_c * a, axis=2)
    a_t = jnp.moveaxis(a_cs, 2, -1)
    seg = a_t[..., :, None] - a_t[..., None, :]
    causal = jnp.tril(jnp.ones((Q, Q), dtype=bool))
    decay_ij = jnp.exp(jnp.where(causal, seg, -jnp.inf))
    cb = jnp.einsum('bcign,bcjgn->bcgij', c_c, b_c)
    y_diag = jnp.einsum('bcgeij,bcjgep->bcigep', cb[:, :, :, None] * decay_ij, xdt)
    a_last = a_cs[:, :, -1]
    decay_to_end = jnp.exp(a_last[:, :, None] - a_cs)
    chunk_states = jnp.einsum('bcjgn,bcjge,bcjgep->bcgepn', b_c, decay_to_end, xdt)

    def carry_state(h, inp):
        chunk_decay, s = inp
        return chunk_decay[..., None, None] * h + s, h

    _, prev_states = lax.scan(carry_state, jnp.zeros_like(chunk_states[:, 0]), (jnp.moveaxis(jnp.exp(a_last), 1, 0), jnp.moveaxis(chunk_states, 1, 0)))
    prev_states = jnp.moveaxis(prev_states, 0, 1)
    y_off = jnp.einsum('bcign,bcgepn->bcigep', c_c, prev_states) * jnp.exp(a_cs)[..., None]
    y = y_diag + y_off + d_skip[:, :, None] * x_c
    return y.reshape(b, L, G * E * P)


def gla_chunked(q, k, v, log_decay):
    b, L, H, dk = q.shape
    dv = v.shape[-1]
    C = GLA_CHUNK
    n = L // C
    q = q.reshape(b, n, C, H, dk)
    k = k.reshape(b, n, C, H, dk)
    v = v.reshape(b, n, C, H, dv)
    gcs = jnp.cumsum(log_decay.reshape(b, n, C, H, dk), axis=2)
    ref = gcs[:, :, C // 2:C // 2 + 1]
    scores = jnp.einsum('bnihd,bnjhd->bnhij', q * jnp.exp(gcs - ref), k * jnp.exp(ref - gcs))
    causal = jnp.tril(jnp.ones((C, C), dtype=bool))
    scores = jnp.where(causal, scores, 0.0)
    o_intra = jnp.einsum('bnhij,bnjhv->bnihv', scores, v)
    g_last = gcs[:, :, -1]
    chunk_kv = jnp.einsum('bnjhd,bnjhv->bnhdv', k * jnp.exp(g_last[:, :, None] - gcs), v)

    def carry_state(s, inp):
        decay, kv = inp
        return decay[..., None] * s + kv, s

    _, prev_states = lax.scan(carry_state, jnp.zeros_like(chunk_kv[:, 0]), (jnp.moveaxis(jnp.exp(g_last), 1, 0), jnp.moveaxis(chunk_kv, 1, 0)))
    prev_states = jnp.moveaxis(prev_states, 0, 1)
    o_inter = jnp.einsum('bnihd,bnhdv->bnihv', q * jnp.exp(gcs), prev_states)
    return (o_intra + o_inter).reshape(b, L, H, dv)


def ssd_gla_layer(x, norm_w, w_in, conv_w, conv_b, a_log, dt_bias, d_skip, ssd_norm_w, gate_w2, gate_b, gla_norm_w, w_out):
    bsz, L, _ = x.shape
    h = rms_norm(x, norm_w)
    u = h @ w_in.astype(jnp.float32)
    z_a, xbc, dt_raw, q_b, k_b, v_b, gate_lr, g_b = split_cols(u, EVEN_SPLITS)
    xbc = jax.nn.silu(causal_depthwise_conv(xbc, conv_w, conv_b))
    xs, bm, cm = split_cols(xbc, (SSD_WIDTH, SSD_GROUPS * SSD_STATE, SSD_GROUPS * SSD_STATE))
    xs = xs.reshape(bsz, L, SSD_GROUPS, SSD_HEADS_PER_GROUP, SSD_HEAD_DIM)
    bm = bm.reshape(bsz, L, SSD_GROUPS, SSD_STATE)
    cm = cm.reshape(bsz, L, SSD_GROUPS, SSD_STATE)
    dt = jax.nn.softplus(dt_raw + dt_bias.astype(jnp.float32)).reshape(bsz, L, SSD_GROUPS, SSD_HEADS_PER_GROUP)
    a = -jnp.exp(a_log.astype(jnp.float32)).reshape(SSD_GROUPS, SSD_HEADS_PER_GROUP)
    d = d_skip.astype(jnp.float32).reshape(SSD_GROUPS, SSD_HEADS_PER_GROUP)
    y_a = ssd_chunked_scan(xs, dt, a, bm, cm, d)
    y_a = rms_norm(y_a * jax.nn.silu(z_a), ssd_norm_w)
    q_b = q_b.reshape(bsz, L, GLA_HEADS, GLA_HEAD_K) * (GLA_HEAD_K ** -0.5)
    k_b = k_b.reshape(bsz, L, GLA_HEADS, GLA_HEAD_K)
    v_b = v_b.reshape(bsz, L, GLA_HEADS, GLA_HEAD_V)
    log_decay = jax.nn.log_sigmoid(gate_lr @ gate_w2.astype(jnp.float32) + gate_b.astype(jnp.float32)) / GLA_GATE_NORMALIZER
    log_decay = log_decay.reshape(bsz, L, GLA_HEADS, GLA_HEAD_K)
    o_b = gla_chunked(q_b, k_b, v_b, log_decay)
    o_b = rms_norm(o_b, gla_norm_w).reshape(bsz, L, GLA_VALUE_WIDTH) * jax.nn.silu(g_b)
    out = jnp.concatenate([y_a, o_b], axis=-1) @ w_out.astype(jnp.float32)
    return x + out.astype(x.dtype)


def moba_attention(q, k, v):
    b, H, L, d = q.shape
    nb = -(-L // MOBA_BLOCK)
    lp = nb * MOBA_BLOCK
    pad = ((0, 0), (0, 0), (0, lp - L), (0, 0))
    q = jnp.pad(q * (d ** -0.5), pad)
    k = jnp.pad(k, pad)
    v = jnp.pad(v, pad)
    kb = k.reshape(b, H, nb, MOBA_BLOCK, d)
    vb = v.reshape(b, H, nb, MOBA_BLOCK, d)
    k_mean = jnp.mean(kb, axis=3)
    gate = jnp.einsum('bhqd,bhnd->bhqn', q, k_mean).astype(jnp.float32)
    n_past_all = jnp.arange(lp) // MOBA_BLOCK
    fully_past = jnp.arange(nb)[None, :] < n_past_all[:, None]
    gate = jnp.where(fully_past, gate, -jnp.inf)
    k_sel = min(MOBA_TOPK, nb)
    _, top_idx = lax.top_k(gate, k_sel)
    bi = jnp.arange(b)[:, None, None]
    hi = jnp.arange(H)[None, :, None]

    def attend_chunk(start):
        qc = lax.dynamic_slice_in_dim(q, start, MOBA_Q_CHUNK, axis=2)
        idx_c = lax.dynamic_slice_in_dim(top_idx, start, MOBA_Q_CHUNK, axis=2)
        pos = start + jnp.arange(MOBA_Q_CHUNK)
        n_past = pos // MOBA_BLOCK
        logits = []
        for s in range(k_sel):
            ks = kb[bi, hi, idx_c[..., s]]
            ls = jnp.einsum('bhqd,bhqkd->bhqk', qc, ks).astype(jnp.float32)
            logits.append(jnp.where((s < n_past)[:, None], ls, -jnp.inf))
        own = start // MOBA_BLOCK
        ko = lax.dynamic_index_in_dim(kb, own, axis=2, keepdims=False)
        vo = lax.dynamic_index_in_dim(vb, own, axis=2, keepdims=False)
        lo = jnp.einsum('bhqd,bhkd->bhqk', qc, ko).astype(jnp.float32)
        key_pos = own * MOBA_BLOCK + jnp.arange(MOBA_BLOCK)
        logits.append(jnp.where(key_pos[None, :] <= pos[:, None], lo, -jnp.inf))
        p = jax.nn.softmax(jnp.concatenate(logits, axis=-1), axis=-1)
        p_parts = jnp.split(p, k_sel + 1, axis=-1)
        out = jnp.einsum('bhqk,bhkd->bhqd', p_parts[-1].astype(vo.dtype), vo)
        for s in range(k_sel):
            vs = vb[bi, hi, idx_c[..., s]]
            out = out + jnp.einsum('bhqk,bhqkd->bhqd', p_parts[s].astype(vs.dtype), vs)
        return out

    starts = jnp.arange(lp // MOBA_Q_CHUNK) * MOBA_Q_CHUNK
    outs = lax.map(attend_chunk, starts)
    out = jnp.moveaxis(outs, 0, 2).reshape(b, H, lp, d)
    return out[:, :, :L]


def moba_layer(x, norm_w, w_in, w_out):
    bsz, L, _ = x.shape
    h = rms_norm(x, norm_w)
    u = h @ w_in.astype(jnp.float32)
    q, k, v, z = split_cols(u, (MOBA_WIDTH, MOBA_WIDTH, MOBA_WIDTH, MOBA_WIDTH))
    to_heads = lambda t: t.reshape(bsz, L, MOBA_HEADS, MOBA_HEAD_DIM).transpose(0, 2, 1, 3)
    o = moba_attention(to_heads(q), to_heads(k), to_heads(v))
    o = o.transpose(0, 2, 1, 3).reshape(bsz, L, MOBA_WIDTH) * jax.nn.silu(z)
    return x + (o @ w_out.astype(jnp.float32)).astype(x.dtype)


def setup_inputs(seed: int = 0) -> dict:
    key = jax.random.key(seed)
    ks = jax.random.split(key, 20)
    nrm = jax.random.normal
    f32 = jnp.float32
    dt_init = jnp.exp(jax.random.uniform(ks[6], (N_EVEN, SSD_HEADS), f32, np.log(1e-3), np.log(1e-1)))
    return {
        'x': nrm(ks[0], (BATCH, SEQ, D_MODEL), f32),
        'even_norm': 1.0 + 0.02 * nrm(ks[1], (N_EVEN, D_MODEL), f32),
        'even_w_in': nrm(ks[2], (N_EVEN, D_MODEL, EVEN_IN_WIDTH), f32) * D_MODEL ** -0.5,
        'even_conv_w': nrm(ks[3], (N_EVEN, SSD_CONV, SSD_CONV_DIM), f32) * SSD_CONV ** -0.5,
        'even_conv_b': 0.02 * nrm(ks[4], (N_EVEN, SSD_CONV_DIM), f32),
        'even_a_log': jnp.log(jax.random.uniform(ks[5], (N_EVEN, SSD_HEADS), f32, 1.0, 16.0)),
        'even_dt_bias': dt_init + jnp.log(-jnp.expm1(-dt_init)),
        'even_d_skip': 1.0 + 0.1 * nrm(ks[7], (N_EVEN, SSD_HEADS), f32),
        'even_ssd_norm': 1.0 + 0.02 * nrm(ks[8], (N_EVEN, SSD_WIDTH), f32),
        'even_gate_w2': nrm(ks[9], (N_EVEN, GLA_GATE_RANK, GLA_KEY_WIDTH), f32) * GLA_GATE_RANK ** -0.5,
        'even_gate_b': 0.1 * nrm(ks[10], (N_EVEN, GLA_KEY_WIDTH), f32),
        'even_gla_norm': 1.0 + 0.02 * nrm(ks[11], (N_EVEN, GLA_HEAD_V), f32),
        'even_w_out': nrm(ks[12], (N_EVEN, EVEN_MIX_WIDTH, D_MODEL), f32) * EVEN_MIX_WIDTH ** -0.5,
        'odd_norm': 1.0 + 0.02 * nrm(ks[13], (N_ODD, D_MODEL), f32),
        'odd_w_in': nrm(ks[14], (N_ODD, D_MODEL, ODD_IN_WIDTH), f32) * D_MODEL ** -0.5,
        'odd_w_out': nrm(ks[15], (N_ODD, MOBA_WIDTH, D_MODEL), f32) * MOBA_WIDTH ** -0.5,
        'final_norm': 1.0 + 0.02 * nrm(ks[16], (D_MODEL,), f32),
    }


def reference(x, even_norm, even_w_in, even_conv_w, even_conv_b, even_a_log, even_dt_bias, even_d_skip, even_ssd_norm, even_gate_w2, even_gate_b, even_gla_norm, even_w_out, odd_norm, odd_w_in, odd_w_out, final_norm):
    for layer in range(DEPTH):
        i = layer // 2
        if layer % 2 == 0:
            x = ssd_gla_layer(x, even_norm[i], even_w_in[i], even_conv_w[i], even_conv_b[i], even_a_log[i], even_dt_bias[i], even_d_skip[i], even_ssd_norm[i], even_gate_w2[i], even_gate_b[i], even_gla_norm[i], even_w_out[i])
        else:
            x = moba_layer(x, odd_norm[i], odd_w_in[i], odd_w_out[i])
    return rms_norm(x, final_norm).astype(x.dtype)
```

```python
import numpy as np
from contextlib import ExitStack
import concourse.bass as bass
import concourse.mybir as mybir
from concourse.bass_utils import run_bass_kernel_spmd

F32 = mybir.dt.float32
BF16 = mybir.dt.bfloat16
AF = mybir.ActivationFunctionType
ALU = mybir.AluOpType
AX = mybir.AxisListType

L = 2048
D = 1024
NT = L // 128
EIN = 5408
C_Z, C_XBC, C_DT, C_Q, C_K, C_V, C_GL, C_G = 0, 1024, 2304, 2320, 2832, 3344, 4368, 4384
EPS = 1e-6
NEG = -1.0e30


class Res:
    __slots__ = ('name', 'lw', 'rd', 'excl')

    def __init__(self, name, excl=False):
        self.name = name; self.lw = None; self.rd = {}; self.excl = excl


class Tok:
    __slots__ = ('key', 'sem', 'val', 'eng')

    def __init__(self, key, sem, val, eng):
        self.key = key; self.sem = sem; self.val = val; self.eng = eng


class Sched:
    ENG = ('pe', 'act', 'dve', 'pool', 'sp')

    def __init__(self, nc, es, n_dma=12, limit=None):
        self.nc = nc
        self.limit = limit; self.nops = 0; self.marks = {}
        self.prog = {e: [] for e in self.ENG}
        self.cnt = {e: 0 for e in self.ENG}
        self.sem = {e: es.enter_context(nc.semaphore(f"s_{e}")) for e in self.ENG}
        self.seen = {e: {} for e in self.ENG}
        self.nd = n_dma
        self.dsem = {q: [es.enter_context(nc.semaphore(f"d_{q}{i}")) for i in range(n_dma)] for q in ('sp', 'pool', 'act')}
        self.duse = {q: [0] * n_dma for q in self.dsem}
        self.dn = {q: 0 for q in self.dsem}

    def _waits(self, e, reads, writes):
        deps = []
        for r in reads:
            if r.lw is not None: deps.append((r.lw, 'raw'))
        for w in writes:
            if w.lw is not None: deps.append((w.lw, 'waw'))
            for t in w.rd.values(): deps.append((t, 'war'))
        need = {}
        for tok, kind in deps:
            if tok.eng == e and e == 'pe':
                continue
            if self.seen[e].get(tok.key, 0) >= tok.val: continue
            if tok.key not in need or need[tok.key].val < tok.val:
                need[tok.key] = tok
        for k, t in need.items():
            self.seen[e][k] = t.val
        return list(need.values())

    def _commit(self, tok, reads, writes):
        for r in reads:
            o = r.rd.get(tok.key)
            if o is None or o.val < tok.val: r.rd[tok.key] = tok
        for w in writes:
            w.lw = tok; w.rd = {}

    def mark(self, name):
        self.marks[name] = self.nops

    def op(self, e, fn, reads=(), writes=()):
        self.nops += 1
        if self.limit is not None and self.nops > self.limit: return None
        reads = [getattr(r, 'r', r) for r in reads]
        writes = [getattr(w, 'r', w) for w in writes]
        writes = writes + [r for r in reads if r.excl and r not in writes]
        reads = [r for r in reads if not r.excl]
        waits = self._waits(e, reads, writes)
        self.cnt[e] += 1
        tok = Tok(e, self.sem[e], self.cnt[e], e)
        self.prog[e].append((waits, fn, self.sem[e], 1))
        self._commit(tok, reads, writes)
        return tok

    def dma(self, q, out, in_, reads=(), writes=(), **kw):
        self.nops += 1
        if self.limit is not None and self.nops > self.limit: return None
        reads = [getattr(r, 'r', r) for r in reads]
        writes = [getattr(w, 'r', w) for w in writes]
        waits = self._waits(q, reads, writes)
        n = self.dn[q]; self.dn[q] += 1
        slot = n % self.nd
        sem = self.dsem[q][slot]
        self.duse[q][slot] += 1
        val = 16 * self.duse[q][slot]
        key = f"d_{q}{slot}"
        if val > 16 and self.seen[q].get(key, 0) < val - 16:
            waits.append(Tok(key, sem, val - 16, None))
            self.seen[q][key] = val - 16
        tok = Tok(key, sem, val, None)
        self.prog[q].append((waits, lambda eng: eng.dma_start(out=out, in_=in_, **kw), sem, 16))
        self._commit(tok, reads, writes)
        return tok

    def wait_tok(self, e, tok):
        if tok is None: return
        if self.seen[e].get(tok.key, 0) >= tok.val: return
        self.seen[e][tok.key] = tok.val
        self.prog[e].append(([tok], None, None, 0))

    def barrier(self):
        if self.limit is not None and self.nops > self.limit: return
        toks = []
        for e in ('pe', 'act', 'dve', 'pool'):
            if self.cnt[e] > 0: toks.append(Tok(e, self.sem[e], self.cnt[e], e))
        for q in self.dsem:
            for i in range(self.nd):
                if self.duse[q][i] > 0: toks.append(Tok(f"d_{q}{i}", self.dsem[q][i], 16 * self.duse[q][i], None))
        for e in self.ENG:
            for t in toks:
                self.wait_tok(e, t)

    def emit(self):
        nc = self.nc
        needed = set()
        for e in self.ENG:
            for waits, fn, sem, inc in self.prog[e]:
                for w in waits:
                    if w.eng is not None:
                        needed.add((w.eng, w.val))
        rank = {}
        for e in self.ENG:
            r = 0; seq = 0
            for waits, fn, sem, inc in self.prog[e]:
                if fn is not None and inc == 1:
                    seq += 1
                    if (e, seq) in needed:
                        r += 1
                        rank[(e, seq)] = r
        with nc.Block() as block:
            def run(name):
                def body(eng):
                    seq = 0
                    for waits, fn, sem, inc in self.prog[name]:
                        for w in waits:
                            v = rank[(w.eng, w.val)] if w.eng is not None else w.val
                            eng.wait_ge(w.sem, v)
                        if fn is not None:
                            ins = fn(eng)
                            if inc == 1:
                                seq += 1
                                if (name, seq) in rank:
                                    ins.then_inc(sem, 1)
                            else:
                                ins.then_inc(sem, inc)
                return body
            block.tensor(run('pe'))
            block.scalar(run('act'))
            block.vector(run('dve'))
            block.gpsimd(run('pool'))
            block.sync(run('sp'))


class T:
    def __init__(self, nc, es, name, shape, dt):
        self.t = es.enter_context(nc.sbuf_tensor("sb_" + name, list(shape), dt))
        self.r = Res(name)

    def __getitem__(self, k):
        return self.t[k]

    @staticmethod
    def view(ap, res):
        o = T.__new__(T); o.t = ap; o.r = res
        return o


def build(stage="full", ntiles=NT, limit=None):
    nc = bass.Bass("TRN2", target_bir_lowering=False)
    dram_in = lambda name, shape: nc.dram_tensor(name, list(shape), F32, kind="ExternalInput").ap()
    x_in = dram_in("x", [L, D])
    w_in0_d = dram_in("w_in0", [D, EIN])
    w_out0_d = dram_in("w_out0", [2 * D, D])
    w_in1_d = dram_in("w_in1", [D, 4 * D])
    w_out1_d = dram_in("w_out1", [D, D])
    norm0_d = dram_in("norm0", [128, 8])
    wo0s_d = dram_in("wo0s", [128, 16])
    norm1_d = dram_in("norm1", [128, 8])
    fnorm_d = dram_in("fnorm", [1, D])
    convw_d = dram_in("convw", [128, 40])
    convb_d = dram_in("convb", [1, 1280])
    hvec_d = dram_in("hvec", [1, 48])
    gw2_d = dram_in("gw2", [16, 512])
    gb_d = dram_in("gb", [128, 4])
    cst_d = dram_in("cst", [128, 5 * 128])
    rmask_d = dram_in("rmask", [128, 512])
    out_d = nc.dram_tensor("out", [L, D], F32, kind="ExternalOutput").ap()
    x1_d = nc.dram_tensor("x1s", [L, D], F32, kind="Internal").ap()

    es = ExitStack()
    with es:
        S = Sched(nc, es, limit=limit)
        mkp = lambda name, shape, dt: T(nc, es, name, shape, dt)
        ARENA_BYTES = 86 * 1024
        arena = es.enter_context(nc.sbuf_tensor("arena", [128, ARENA_BYTES // 4], F32))
        aoff = [0]

        def mk(name, shape, dt):
            n = int(np.prod(shape[1:])); esz = 4 if dt == F32 else 2
            nb = ((n * esz + 31) // 32) * 32
            f0 = aoff[0] // 4; aoff[0] += nb
            assert aoff[0] <= ARENA_BYTES, ("arena overflow", name, aoff[0])
            v = arena[0:shape[0], f0:f0 + nb // 4]
            if dt == BF16: v = v.bitcast(BF16)
            v = v[:, 0:n]
            if len(shape) == 3: v = v.rearrange("p (a b) -> p a b", a=shape[1])
            if len(shape) == 4: v = v.rearrange("p (a b c) -> p a b c", a=shape[1], b=shape[2])
            return T.view(v, Res(name))
        pst = es.enter_context(nc.psum_tensor("pst", [128, 8, 512], F32))
        PB = [Res(f"bank{i}", excl=True) for i in range(8)]

        def pf(b, n=1):
            return pst[:, b:b + n, :].rearrange("p b f -> p (b f)")

        def pb16(b):
            return pst[:, b, :].bitcast(BF16)

        cst = mkp("cst", [128, 640], F32)
        S.dma('sp', cst[:], cst_d[:, :], writes=[cst])
        ident_f = cst[:, 0:128]; tri_f = cst[:, 128:256]; ones_f = cst[:, 256:384]
        cstb = mkp("cstb", [128, 640], BF16)
        S.op('dve', lambda e: e.tensor_copy(cstb[:], cst[:]), reads=[cst], writes=[cstb])
        ident_b = cstb[:, 0:128]; ones_b = cstb[:, 256:384]; mneg_b = cstb[:, 384:512]; mbd_b = cstb[:, 512:640]
        rmask = mk("rmask", [128, 512], BF16)
        S.dma('pool', rmask[:], rmask_d[:, :], writes=[rmask])

        w_in0 = mkp("w_in0", [128, 8, EIN], BF16)
        w_out0 = mkp("w_out0", [128, 16, D], BF16)
        w0v = w_in0_d.rearrange("(k p) n -> p k n", p=128)
        col_chunks = [(c, min(c + 512, EIN)) for c in range(0, EIN, 512)]
        for (c0, c1) in col_chunks:
            S.dma('pool', w_in0[:, :, c0:c1], w0v[:, :, c0:c1], writes=[w_in0])
        wo0v = w_out0_d.rearrange("(k p) n -> p k n", p=128)
        for k0 in range(0, 16, 4):
            S.dma('pool', w_out0[:, k0:k0 + 4, :], wo0v[:, k0:k0 + 4, :], writes=[w_out0])
        norm0 = mk("norm0", [128, 8], F32)
        S.dma('sp', norm0[:], norm0_d[:, :], writes=[norm0])
        wo0s = mk("wo0s", [128, 16], F32)
        S.dma('sp', wo0s[:], wo0s_d[:, :], writes=[wo0s])
        for k in range(8):
            S.op('dve', lambda e, k=k: e.tensor_scalar(w_in0[:, k, :], w_in0[:, k, :], norm0[:, k:k + 1], None, ALU.mult),
                 reads=[w_in0, norm0], writes=[w_in0])
        for k in range(16):
            S.op('pool', lambda e, k=k: e.tensor_scalar(w_out0[:, k, :], w_out0[:, k, :], wo0s[:, k:k + 1], None, ALU.mult),
                 reads=[w_out0, wo0s], writes=[w_out0])
        convw = mk("convw", [128, 40], F32)
        S.dma('sp', convw[:], convw_d[:, :], writes=[convw])
        convb = mk("convb", [1, 1280], BF16)
        S.dma('pool', convb[:], convb_d[:, :], writes=[convb])
        cdiag = mk("cdiag", [128, 40, 128], BF16)
        for i in range(40):
            S.op('dve', lambda e, i=i: e.tensor_scalar(cdiag[:, i, :], ident_f, convw[:, i:i + 1], None, ALU.mult),
                 reads=[cst, convw], writes=[cdiag])
        hvec = mk("hvec", [128, 48], F32)
        S.dma('sp', hvec[:], hvec_d.partition_broadcast(128), writes=[hvec])
        a_bc = mk("a_bc", [128, 16], F32)
        S.op('act', lambda e: e.activation(a_bc[:], hvec[:, 0:16], AF.Exp), reads=[hvec], writes=[a_bc])
        S.op('dve', lambda e: e.tensor_scalar(a_bc[:], a_bc[:], -1.0, None, ALU.mult), reads=[a_bc], writes=[a_bc])
        dtb_bc = hvec[:, 16:32]
        DI = mk("DI", [128, 16, 128], BF16)
        for h in range(16):
            S.op('dve', lambda e, h=h: e.tensor_scalar(DI[:, h, :], ident_f, hvec[:, 32 + h:33 + h], None, ALU.mult),
                 reads=[cst, hvec], writes=[DI])
        gw2 = mk("gw2", [16, 512], BF16)
        S.dma('pool', gw2[:], gw2_d[:, :], writes=[gw2])
        ngb = mk("ngb", [128, 4], F32)
        S.dma('sp', ngb[:], gb_d[:, :], writes=[ngb])
        S.op('dve', lambda e: e.tensor_scalar(ngb[:], ngb[:], -1.0, None, ALU.mult), reads=[ngb], writes=[ngb])

        P_x = mk("P_x", [128, D], F32)
        P_hb = mk("P_hb", [128, D], BF16)
        P_hT = mk("P_hT", [128, 8, 128], BF16)
        P_zg = mk("P_zg", [128, D], F32)
        P_v = mk("P_v", [128, D], BF16)
        P_U = mk("P_U", [128, 10, 131], BF16)
        P_halo = mk("P_halo", [128, 10, 3], BF16)
        P_xc = mk("P_xc", [128, 10, 128], BF16)
        P_mix = mk("P_mix", [128, 2 * D], BF16)
        P_q = mk("P_q", [128, 512], F32)
        P_k = mk("P_k", [128, 512], F32)
        P_F1 = mk("P_F1", [128, D], F32)
        P_F2 = mk("P_F2", [128, D], F32)
        P_F3 = mk("P_F3", [128, D], F32)
        P_F4 = mk("P_F4", [128, 512], F32)
        P_scm = mk("P_scm", [128, 8, 128], BF16)
        P_qd = mk("P_qd", [128, 4, 128], BF16)
        P_kd = mk("P_kd", [128, 4, 128], BF16)
        P_kl = mk("P_kl", [128, 4, 128], BF16)
        P_qg = mk("P_qg", [128, 4, 128], BF16)
        glr = mk("glr", [16, 128], BF16)
        Btok = mk("Btok", [128, 128], BF16)
        sm = mk("sm", [128, 256], F32)
        sm_r = {}

        def smv(name, c0, n):
            if name not in sm_r: sm_r[name] = Res("sm_" + name)

            class V:
                r = sm_r[name]
                ap = sm[:, c0:c0 + n]
            return V
        ss_ = smv("ss", 0, 1); rstd_ = smv("rstd", 1, 1)
        dtb_ = smv("dtb", 16, 16); dt_ = smv("dt", 32, 16); dA_ = smv("dA", 48, 16)
        nacs_ = smv("nacs", 64, 16); eacs_ = smv("eacs", 80, 16); dd_ = smv("dd", 96, 16)
        wdt_ = smv("wdt", 112, 16); eal_ = smv("eal", 128, 8); ssy_ = smv("ssy", 136, 1)
        rsy_ = smv("rsy", 137, 1); edec_ = smv("edec", 144, 8); sso_ = smv("sso", 152, 8)
        rso_ = smv("rso", 160, 8)
        Sst = mk("Sst", [128, 512], F32); Sbf = mk("Sbf", [128, 512], BF16)
        GS = mk("GS", [128, 4, 128], F32); GSb0 = mk("GSb0", [128, 4, 128], BF16); GSb1 = mk("GSb1", [128, 4, 128], BF16)
        S.op('dve', lambda e: e.memset(Sst[:], 0.0), writes=[Sst])
        S.op('dve', lambda e: e.memset(Sbf[:], 0.0), writes=[Sbf])
        S.op('dve', lambda e: e.memset(GS[:], 0.0), writes=[GS])
        S.op('dve', lambda e: e.memset(GSb0[:], 0.0), writes=[GSb0])
        S.op('dve', lambda e: e.memset(P_halo[:], 0.0), writes=[P_halo])

        def mm(out, lhsT, rhs, start, stop, R, W, **kw):
            S.op('pe', lambda e: e.matmul(out, lhsT, rhs, start=start, stop=stop, **kw), reads=R, writes=W)

        def rms_rstd(src_ap, src_r, ss_v, rs_v, n, scratch_ap, scratch_r):
            S.op('act', lambda e: e.activation(scratch_ap, src_ap, AF.Square, accum_out=ss_v.ap), reads=[src_r], writes=[scratch_r, ss_v])
            S.op('dve', lambda e: e.tensor_scalar(rs_v.ap, ss_v.ap, 1.0 / n, EPS, ALU.mult, ALU.add), reads=[ss_v], writes=[rs_v])
            S.op('act', lambda e: e.activation(rs_v.ap, rs_v.ap, AF.Ln), reads=[rs_v], writes=[rs_v])
            S.op('act', lambda e: e.activation(rs_v.ap, rs_v.ap, AF.Exp, scale=-0.5), reads=[rs_v], writes=[rs_v])

        def norm_transpose(xsrc, bank):
            rms_rstd(xsrc[:], xsrc, ss_, rstd_, D, P_F1[:], P_F1)
            S.op('dve', lambda e: e.tensor_scalar(P_hb[:], xsrc[:], rstd_.ap, None, ALU.mult), reads=[xsrc, rstd_], writes=[P_hb])
            for k in range(8):
                S.op('pe', lambda e, k=k: e.transpose(pb16(bank)[:, k * 128:(k + 1) * 128], P_hb[:, k * 128:(k + 1) * 128], ident_b),
                     reads=[P_hb, cstb], writes=[PB[bank]])
            S.op('act', lambda e: e.activation(P_hT[:].rearrange("p k t -> p (k t)"), pb16(bank), AF.Copy), reads=[PB[bank]], writes=[P_hT])

        def proj_tok(w, c0, n, bank):
            for k in range(8):
                mm(pf(bank)[:, 0:n], P_hT[:, k, :], w[:, k, c0:c0 + n], k == 0, k == 7, [P_hT, w], [PB[bank]])

        def proj_fm(w, c0, m, bank, slot):
            for k in range(8):
                mm(pf(bank)[0:m, slot * 128:(slot + 1) * 128], w[:, k, c0:c0 + m], P_hT[:, k, :], (k == 0 and slot == 0), k == 7,
                   [P_hT, w], [PB[bank]], skip_group_check=True)

        C1 = 1.0 / 16.0
        LN8 = float(np.log(0.125))

        X1R = [Res(f'x1_{t}') for t in range(NT)]
        S.mark('setup_done')

        def layer0_tile(t):
            S.dma('sp', P_x[:], x_in[t * 128:(t + 1) * 128, :], writes=[P_x])
            norm_transpose(P_x, 0)
            S.mark(f'l0_{t}_projfm')
            for s in range(4): proj_fm(w_in0, C_Q + s * 128, 128, 1, s)
            S.op('act', lambda e: e.activation(P_q[:], pf(1), AF.Copy), reads=[PB[1]], writes=[P_q])
            for s in range(4): proj_fm(w_in0, C_K + s * 128, 128, 2, s)
            S.op('dve', lambda e: e.tensor_copy(P_k[:], pf(2)), reads=[PB[2]], writes=[P_k])
            proj_fm(w_in0, C_GL, 16, 3, 0)
            S.op('dve', lambda e: e.tensor_copy(glr[:], pf(3)[0:16, 0:128]), reads=[PB[3]], writes=[glr])
            S.op('dve', lambda e: e.tensor_copy(P_U[:, :, 0:3], P_halo[:]), reads=[P_halo], writes=[P_U])
            for grp, bank in ((0, 1), (1, 2), (2, 3)):
                nch = 4 if grp < 2 else 2
                for s in range(nch): proj_fm(w_in0, C_XBC + (grp * 4 + s) * 128, 128, bank, s)
                eng = 'act' if grp != 1 else 'dve'
                src = pf(bank)[:, 0:nch * 128].rearrange("p (c t) -> p c t", c=nch)
                dst = P_U[:, grp * 4:grp * 4 + nch, 3:131]
                if eng == 'act':
                    S.op('act', lambda e, src=src, dst=dst: e.activation(dst, src, AF.Copy), reads=[PB[bank]], writes=[P_U])
                else:
                    S.op('dve', lambda e, src=src, dst=dst: e.tensor_copy(dst, src), reads=[PB[bank]], writes=[P_U])
            S.op('dve', lambda e: e.tensor_copy(P_halo[:], P_U[:, :, 128:131]), reads=[P_U], writes=[P_halo])
            proj_tok(w_in0, C_DT, 16, 3)
            S.op('dve', lambda e: e.tensor_tensor(dtb_.ap, pf(3)[:, 0:16], dtb_bc, ALU.add), reads=[PB[3], hvec], writes=[dtb_])
            S.mark(f'l0_{t}_glaprep')
            for ch in range(4):
                mm(pf(1)[:, ch * 128:(ch + 1) * 128], gw2[0:16, ch * 128:(ch + 1) * 128], glr[:], ch == 0, True, [gw2, glr], [PB[1]], skip_group_check=True)
            el = P_F2[:, 0:512]; Nn = P_F2[:, 512:1024]
            for ch in range(4):
                S.op('act', lambda e, ch=ch: e.activation(el[:, ch * 128:(ch + 1) * 128], pf(1)[:, ch * 128:(ch + 1) * 128], AF.Exp, scale=-1.0, bias=ngb[:, ch:ch + 1]),
                     reads=[PB[1], ngb], writes=[P_F2])
            S.op('act', lambda e: e.activation(el, el, AF.Ln, bias=1.0), reads=[P_F2], writes=[P_F2])
            S.op('act', lambda e: e.activation(dt_.ap, dtb_.ap, AF.Exp), reads=[dtb_], writes=[dt_])
            S.op('act', lambda e: e.activation(dt_.ap, dt_.ap, AF.Ln, bias=1.0), reads=[dt_], writes=[dt_])
            S.op('dve', lambda e: e.tensor_tensor_scan(Nn, rmask[:], el, 0.0, ALU.mult, ALU.add), reads=[rmask, P_F2], writes=[P_F2])
            N3 = Nn.rearrange("p (s t) -> p s t", t=64)
            Dref = P_F3[:, 0:512]; Dlast = P_F3[:, 512:1024]
            S.op('dve', lambda e: e.tensor_tensor(Dref.rearrange("p (s t) -> p s t", t=64), N3, N3[:, :, 32:33].to_broadcast([128, 8, 64]), ALU.subtract),
                 reads=[P_F2], writes=[P_F3])
            S.op('dve', lambda e: e.tensor_tensor(Dlast.rearrange("p (s t) -> p s t", t=64), N3[:, :, 63:64].to_broadcast([128, 8, 64]), N3, ALU.subtract),
                 reads=[P_F2], writes=[P_F3])
            E2 = P_F4[:]
            S.op('act', lambda e: e.activation(E2, Dref, AF.Exp, scale=C1), reads=[P_F3], writes=[P_F4])
            S.op('act', lambda e: e.activation(Dref, Dref, AF.Exp, scale=-C1, bias=LN8), reads=[P_F3], writes=[P_F3])
            S.op('act', lambda e: e.activation(Dlast, Dlast, AF.Exp, scale=-C1), reads=[P_F3], writes=[P_F3])
            S.op('act', lambda e: e.activation(edec_.ap, N3[:, :, 63], AF.Exp, scale=-C1), reads=[P_F2], writes=[edec_])
            S.op('act', lambda e: e.activation(Nn, Nn, AF.Exp, scale=-C1, bias=LN8), reads=[P_F2], writes=[P_F2])
            f4 = lambda ap: ap.rearrange("p c t -> p (c t)")
            S.op('dve', lambda e: e.tensor_tensor(f4(P_qd[:]), P_q[:], Dref, ALU.mult), reads=[P_q, P_F3], writes=[P_qd])
            S.op('dve', lambda e: e.tensor_tensor(f4(P_kd[:]), P_k[:], E2, ALU.mult), reads=[P_k, P_F4], writes=[P_kd])
            S.op('dve', lambda e: e.tensor_tensor(f4(P_kl[:]), P_k[:], Dlast, ALU.mult), reads=[P_k, P_F3], writes=[P_kl])
            S.op('dve', lambda e: e.tensor_tensor(f4(P_qg[:]), P_q[:], Nn, ALU.mult), reads=[P_q, P_F2], writes=[P_qg])
            S.mark(f'l0_{t}_conv')
            for grp, bank in ((0, 1), (1, 2), (2, 3)):
                nch = 4 if grp < 2 else 2
                for s in range(nch):
                    c = grp * 4 + s
                    o = pf(bank)[:, s * 128:(s + 1) * 128]
                    for k in range(4):
                        mm(o, cdiag[:, c * 4 + k, :], P_U[:, c, k:k + 128], (k == 0 and s == 0), False, [cdiag, P_U], [PB[bank]], skip_group_check=True)
                    mm(o, convb[0:1, c * 128:(c + 1) * 128], ones_b[0:1, :], False, True, [convb, cstb], [PB[bank]], skip_group_check=True)
                S.op('act', lambda e, grp=grp, nch=nch, bank=bank: e.activation(
                    P_xc[:, grp * 4:grp * 4 + nch, :].rearrange("p c t -> p (c t)"), pf(bank)[:, 0:nch * 128], AF.Silu),
                    reads=[PB[bank]], writes=[P_xc])
            for half in range(2):
                proj_tok(w_in0, C_Z + half * 512, 512, 4 + half)
            S.op('act', lambda e: e.activation(P_zg[:], pf(4, 2), AF.Silu), reads=[PB[4], PB[5]], writes=[P_zg])
            S.mark(f'l0_{t}_ssd')
            xs_tok = P_F4[:].bitcast(BF16)
            for c in range(8):
                S.op('pe', lambda e, c=c: e.transpose(pb16(0)[:, c * 128:(c + 1) * 128], P_xc[:, c, :], ident_b), reads=[P_xc, cstb], writes=[PB[0]])
            S.op('dve', lambda e: e.tensor_copy(xs_tok, pb16(0)), reads=[PB[0]], writes=[P_F4])
            S.op('pe', lambda e: e.transpose(pb16(0)[:, 0:128], P_xc[:, 8, :], ident_b), reads=[P_xc, cstb], writes=[PB[0]])
            S.op('act', lambda e: e.activation(Btok[:], pb16(0)[:, 0:128], AF.Copy), reads=[PB[0]], writes=[Btok])
            S.op('dve', lambda e: e.tensor_tensor(dA_.ap, dt_.ap, a_bc[:], ALU.mult), reads=[dt_, a_bc], writes=[dA_])
            mm(pf(1)[:, 0:16], tri_f, dA_.ap, True, True, [cst, dA_], [PB[1]], skip_group_check=True)
            mm(pf(1)[:, 16:32], ones_f, dA_.ap, False, True, [cst, dA_], [PB[1]], skip_group_check=True)
            S.op('dve', lambda e: e.tensor_scalar(nacs_.ap, pf(1)[:, 0:16], -1.0, None, ALU.mult), reads=[PB[1]], writes=[nacs_])
            S.op('act', lambda e: e.activation(eacs_.ap, pf(1)[:, 0:16], AF.Exp), reads=[PB[1]], writes=[eacs_])
            S.op('dve', lambda e: e.tensor_tensor(dd_.ap, pf(1)[:, 16:32], nacs_.ap, ALU.add), reads=[PB[1], nacs_], writes=[dd_])
            S.op('act', lambda e: e.activation(dd_.ap, dd_.ap, AF.Exp), reads=[dd_], writes=[dd_])
            S.op('dve', lambda e: e.tensor_tensor(wdt_.ap, dd_.ap, dt_.ap, ALU.mult), reads=[dd_, dt_], writes=[wdt_])
            S.op('act', lambda e: e.activation(eal_.ap[0:64, :], pf(1)[0:64, 16:24], AF.Exp), reads=[PB[1]], writes=[eal_])
            S.op('act', lambda e: e.activation(eal_.ap[64:128, :], pf(1)[64:128, 24:32], AF.Exp), reads=[PB[1]], writes=[eal_])
            xdt = P_q[:].bitcast(BF16); xw = P_k[:].bitcast(BF16)
            h3 = lambda ap: ap.rearrange("p (h d) -> p h d", h=16)
            S.op('dve', lambda e: e.tensor_tensor(h3(xdt), h3(xs_tok), dt_.ap.unsqueeze(2).to_broadcast([128, 16, 64]), ALU.mult),
                 reads=[P_F4, dt_, P_qd, P_qg], writes=[P_q])
            S.op('dve', lambda e: e.tensor_tensor(h3(xw), h3(xs_tok), wdt_.ap.unsqueeze(2).to_broadcast([128, 16, 64]), ALU.mult),
                 reads=[P_F4, wdt_, P_kd, P_kl], writes=[P_k])
            for g in range(2):
                mm(pf(2 + g)[:, 0:128], P_xc[g * 64:(g + 1) * 64, 8, :], P_xc[g * 64:(g + 1) * 64, 9, :], True, True,
                   [P_xc], [PB[2 + g]])
            S.mark(f'l0_{t}_segsum')
            EG = P_F3[:].bitcast(BF16).rearrange("p (h i) -> p h i", h=16)
            for h in range(16):
                bank = 4 + h // 4
                o = pf(bank)[:, (h % 4) * 128:(h % 4 + 1) * 128]
                mm(o, dA_.ap[:, h:h + 1].to_broadcast([128, 128]), tri_f, h % 4 == 0, False, [dA_, cst], [PB[bank]], skip_group_check=True)
                mm(o, ident_b, mneg_b, False, True, [cstb], [PB[bank]], skip_group_check=True)
                S.op('act', lambda e, h=h, o=o: e.activation(EG[:, h, :], o, AF.Exp, bias=nacs_.ap[:, h:h + 1]),
                     reads=[PB[bank], nacs_, P_qd, P_kl], writes=[P_F3])
            for g in range(2):
                S.op('dve', lambda e, g=g: e.tensor_tensor(EG[:, g * 8:(g + 1) * 8, :], EG[:, g * 8:(g + 1) * 8, :],
                                                           pf(2 + g)[:, 0:128].unsqueeze(1).to_broadcast([128, 8, 128]), ALU.mult),
                     reads=[P_F3, PB[2 + g]], writes=[P_F3])
            S.mark(f'l0_{t}_ydiag')
            for h in range(16):
                bank = 4 + h // 8
                o = pf(bank)[:, (h % 8) * 64:(h % 8 + 1) * 64]
                mm(o, EG[:, h, :], xdt[:, h * 64:(h + 1) * 64], h % 8 == 0, False, [P_F3, P_q], [PB[bank]], skip_group_check=True)
                mm(o, DI[:, h, :], xs_tok[:, h * 64:(h + 1) * 64], False, True, [DI, P_F4], [PB[bank]], skip_group_check=True)
            for g in range(2):
                mm(pf(6 + g), P_xc[g * 64:(g + 1) * 64, 9, :], Sbf[g * 64:(g + 1) * 64, :], True, True, [P_xc, Sbf], [PB[6 + g]])
            S.op('dve', lambda e: e.tensor_tensor(h3(P_F1[:]), h3(pf(6, 2)), eacs_.ap.unsqueeze(2).to_broadcast([128, 16, 64]), ALU.mult),
                 reads=[PB[6], PB[7], eacs_], writes=[P_F1])
            S.op('dve', lambda e: e.tensor_tensor(P_F1[:], P_F1[:], pf(4, 2), ALU.add), reads=[P_F1, PB[4], PB[5]], writes=[P_F1])
            S.op('pool', lambda e: e.tensor_tensor(P_F1[:], P_F1[:], P_zg[:], ALU.mult), reads=[P_F1, P_zg], writes=[P_F1])
            rms_rstd(P_F1[:], P_F1, ssy_, rsy_, D, P_F2[:], P_F2)
            S.op('dve', lambda e: e.tensor_scalar(P_mix[:, 0:D], P_F1[:], rsy_.ap, None, ALU.mult), reads=[P_F1, rsy_], writes=[P_mix])
            for g in range(2):
                mm(pf(1)[g * 64:(g + 1) * 64, :], Btok[:, g * 64:(g + 1) * 64], xw[:, g * 512:(g + 1) * 512], True, True, [Btok, P_k], [PB[1]], skip_group_check=True)
            S.op('dve', lambda e: e.tensor_tensor(Sst[:].rearrange("p (h d) -> p h d", h=8), Sst[:].rearrange("p (h d) -> p h d", h=8),
                                                  eal_.ap.unsqueeze(2).to_broadcast([128, 8, 64]), ALU.mult), reads=[Sst, eal_], writes=[Sst])
            S.op('dve', lambda e: e.tensor_tensor(Sst[:], Sst[:], pf(1), ALU.add), reads=[Sst, PB[1]], writes=[Sst])
            S.op('pool', lambda e: e.tensor_copy(Sbf[:], Sst[:]), reads=[Sst], writes=[Sbf])
            S.mark(f'l0_{t}_vg')
            for half in range(2):
                proj_tok(w_in0, C_V + half * 512, 512, 2 + half)
            S.op('act', lambda e: e.activation(P_v[:], pf(2, 2), AF.Copy), reads=[PB[2], PB[3]], writes=[P_v])
            for half in range(2):
                proj_tok(w_in0, C_G + half * 512, 512, 2 + half)
            S.op('act', lambda e: e.activation(P_zg[:], pf(2, 2), AF.Silu), reads=[PB[2], PB[3]], writes=[P_zg])
            S.mark(f'l0_{t}_glamain')
            kl_tok = P_hb[:, 0:512]
            for ch in range(4):
                S.op('pe', lambda e, ch=ch: e.transpose(pb16(0)[:, ch * 128:(ch + 1) * 128], P_kl[:, ch, :], ident_b), reads=[P_kl, cstb], writes=[PB[0]])
            S.op('dve', lambda e: e.tensor_copy(kl_tok, pb16(0)[:, 0:512]), reads=[PB[0]], writes=[P_hb])
            for h in range(8):
                ch, hh = h // 2, h % 2
                bank = 4 + hh
                mm(pf(bank)[:, ch * 128:(ch + 1) * 128], P_kd[hh * 64:(hh + 1) * 64, ch, :], P_qd[hh * 64:(hh + 1) * 64, ch, :],
                   ch == 0, True, [P_kd, P_qd], [PB[bank]], skip_group_check=True)
            scm4 = P_scm[:].rearrange("p (c two) i -> p two c i", two=2)
            for hh in range(2):
                S.op('dve', lambda e, hh=hh: e.tensor_tensor(scm4[:, hh, :, :], pf(4 + hh).rearrange("p (c i) -> p c i", c=4),
                                                             mbd_b.unsqueeze(1).to_broadcast([128, 4, 128]), ALU.mult), reads=[PB[4 + hh], cstb], writes=[P_scm])

            def o_ap(h, p0, p1):
                return pf(6 + h % 2)[p0:p1, (h // 2) * 128:(h // 2 + 1) * 128]
            for h in range(8):
                mm(o_ap(h, 0, 128), P_scm[:, h, :], P_v[:, h * 128:(h + 1) * 128], h < 2, False, [P_scm, P_v], [PB[6 + h % 2]], skip_group_check=True)
            for c in range(2):
                gsb = GSb0 if c == 0 else GSb1
                for h in range(8):
                    ch, hh = h // 2, h % 2
                    mm(o_ap(h, c * 64, (c + 1) * 64), P_qg[hh * 64:(hh + 1) * 64, ch, c * 64:(c + 1) * 64], gsb[hh * 64:(hh + 1) * 64, ch, :],
                       False, True, [P_qg, gsb], [PB[6 + h % 2]], skip_group_check=True)
                for h in range(8):
                    ch, hh = h // 2, h % 2
                    mm(pf(1 + c)[hh * 64:(hh + 1) * 64, ch * 128:(ch + 1) * 128], kl_tok[c * 64:(c + 1) * 64, h * 64:(h + 1) * 64],
                       P_v[c * 64:(c + 1) * 64, h * 128:(h + 1) * 128], h < 2, True, [P_hb, P_v], [PB[1 + c]], skip_group_check=True)
                for ch in range(4):
                    S.op('dve', lambda e, ch=ch, c=c: e.scalar_tensor_tensor(GS[:, ch, :], GS[:, ch, :], edec_.ap[:, ch * 2 + c:ch * 2 + c + 1],
                                                                             pf(1 + c)[:, ch * 128:(ch + 1) * 128], ALU.mult, ALU.add),
                         reads=[GS, edec_, PB[1 + c]], writes=[GS])
                nxt = GSb1 if c == 0 else GSb0
                S.op('pool', lambda e, nxt=nxt: e.tensor_copy(nxt[:], GS[:]), reads=[GS], writes=[nxt])
            F1h = P_F1[:].rearrange("p (c two v) -> p two c v", two=2, v=128)
            F2h = P_F2[:].rearrange("p (c two v) -> p two c v", two=2, v=128)
            for hh in range(2):
                src = pf(6 + hh).rearrange("p (c v) -> p c v", c=4)
                S.op('act', lambda e, hh=hh, src=src: e.activation(F1h[:, hh, :, :], src, AF.Copy), reads=[PB[6 + hh]], writes=[P_F1])
                S.op('act', lambda e, hh=hh, src=src: e.activation(F2h[:, hh, :, :], src, AF.Square), reads=[PB[6 + hh]], writes=[P_F2])
            S.op('dve', lambda e: e.tensor_reduce(sso_.ap, P_F2[:].rearrange("p (h v) -> p h v", h=8), AX.X, ALU.add), reads=[P_F2], writes=[sso_])
            S.op('dve', lambda e: e.tensor_scalar(rso_.ap, sso_.ap, 1.0 / 128, EPS, ALU.mult, ALU.add), reads=[sso_], writes=[rso_])
            S.op('act', lambda e: e.activation(rso_.ap, rso_.ap, AF.Ln), reads=[rso_], writes=[rso_])
            S.op('act', lambda e: e.activation(rso_.ap, rso_.ap, AF.Exp, scale=-0.5), reads=[rso_], writes=[rso_])
            S.op('pool', lambda e: e.tensor_tensor(P_F1[:], P_F1[:], P_zg[:], ALU.mult), reads=[P_F1, P_zg], writes=[P_F1])
            S.op('dve', lambda e: e.tensor_tensor(P_mix[:, D:2 * D].rearrange("p (h v) -> p h v", h=8), P_F1[:].rearrange("p (h v) -> p h v", h=8),
                                                  rso_.ap.unsqueeze(2).to_broadcast([128, 8, 128]), ALU.mult), reads=[P_F1, rso_], writes=[P_mix])
            S.mark(f'l0_{t}_outproj')
            mixT = P_F3[:].bitcast(BF16).rearrange("p (k t) -> p k t", k=16)
            for half in range(2):
                for kk in range(8):
                    k = half * 8 + kk
                    S.op('pe', lambda e, k=k, kk=kk: e.transpose(pb16(0)[:, kk * 128:(kk + 1) * 128], P_mix[:, k * 128:(k + 1) * 128], ident_b),
                         reads=[P_mix, cstb], writes=[PB[0]])
                dst = mixT[:, half * 8:(half + 1) * 8, :].rearrange("p k t -> p (k t)")
                if half == 0:
                    S.op('dve', lambda e, dst=dst: e.tensor_copy(dst, pb16(0)), reads=[PB[0]], writes=[P_F3])
                else:
                    S.op('act', lambda e, dst=dst: e.activation(dst, pb16(0), AF.Copy), reads=[PB[0]], writes=[P_F3])
            for n in range(2):
                for k in range(16):
                    mm(pf(4 + n), mixT[:, k, :], w_out0[:, k, n * 512:(n + 1) * 512], k == 0, k == 15, [P_F3, w_out0], [PB[4 + n]])
            S.op('dve', lambda e: e.tensor_tensor(P_x[:], P_x[:], pf(4, 2), ALU.add), reads=[P_x, PB[4], PB[5]], writes=[P_x])
            dst = out_d if stage == "l0" else x1_d
            return S.dma('sp', dst[t * 128:(t + 1) * 128, :], P_x[:], reads=[P_x], writes=[X1R[t]])

        last = None
        for t in range(ntiles):
            last = layer0_tile(t)
        if stage != "l0":
            last = None
            wflat = w_in0[:].rearrange("p k n -> p (k n)")
            w_in1 = T.view(wflat[:, 0:32768].rearrange("p (k n) -> p k n", k=8), w_in0.r)
            w1v = w_in1_d.rearrange("(k p) n -> p k n", p=128)
            for c0 in range(0, 4096, 512):
                S.dma('pool', w_in1[:, :, c0:c0 + 512], w1v[:, :, c0:c0 + 512], writes=[w_in1])
            oflat = w_out0[:].rearrange("p k n -> p (k n)")
            w_out1 = T.view(oflat[:, 0:8192].rearrange("p (k n) -> p k n", k=8), w_out0.r)
            wo1v = w_out1_d.rearrange("(k p) n -> p k n", p=128)
            for k0 in range(0, 8, 4):
                S.dma('pool', w_out1[:, k0:k0 + 4, :], wo1v[:, k0:k0 + 4, :], writes=[w_out1])
            S.barrier()
            S.mark('l1_start')
            aoff[0] = 0
            VB_hi = T.view(wflat[:, 32768:32768 + 8320].rearrange("p (t h e) -> p t h e", t=8, h=16), Res("VB_hi"))
            KT_hi = T.view(oflat[:, 8192:16384].rearrange("p (c t) -> p c t", c=4), Res("KT_hi"))
            KT_lo = mk("KT_lo", [128, 4, L], BF16)
            VB_lo = mk("VB_lo", [128, 8, 16, 65], BF16)
            ksum2 = mk("ksum2", [128, 16, 8], F32)
            kmean = mk("kmean", [128, 8, 8], F32)
            fn_bc = mk("fn_bc", [128, D], F32)
            norm1 = mk("norm1", [128, 8], F32)
            X1t = mk("P_x1", [128, D], F32)
            HB1 = mk("P_hb1", [128, D], BF16)
            HT1 = mk("P_hT1", [128, 8, 128], BF16)
            F11 = mk("P_F11", [128, D], F32)
            ZG1 = mk("P_zg1", [128, D], F32)
            qT = mk("qT", [128, 8, 128], BF16)
            qTf = mk("qTf", [128, 8, 128], F32)
            PT = [mk("PT0", [128, 2, 512], BF16), mk("PT1", [128, 2, 512], BF16)]
            acc = mk("acc", [128, 16, 65], F32)
            accR = [Res(f"acc{c}") for c in range(8)]
            gm = mk("gm", [128, 16, 8], F32)
            top8 = mk("top8", [128, 16, 8], F32)
            sel = mk("sel", [128, 16, 8], F32)
            rden = mk("rden", [128, 16], F32)
            mixo = mk("mixo", [128, D], BF16)
            mixoT = mk("mixoT", [128, 8, 128], BF16)
            sm1 = mk("sm1", [128, 8], F32)
            ss1 = T.view(sm1[:, 0:1], Res("ss1")); rs1 = T.view(sm1[:, 1:2], Res("rs1"))

            class V1:
                pass
            ssv = V1(); ssv.ap = ss1.t; ssv.r = ss1.r
            rsv = V1(); rsv.ap = rs1.t; rsv.r = rs1.r

            def norm_transpose1(bank):
                rms_rstd(X1t[:], X1t, ssv, rsv, D, F11[:], F11)
                S.op('dve', lambda e: e.tensor_scalar(HB1[:], X1t[:], rsv.ap, None, ALU.mult), reads=[X1t, rsv], writes=[HB1])
                for k in range(8):
                    S.op('pe', lambda e, k=k: e.transpose(pb16(bank)[:, k * 128:(k + 1) * 128], HB1[:, k * 128:(k + 1) * 128], ident_b),
                         reads=[HB1, cstb], writes=[PB[bank]])
                S.op('act', lambda e: e.activation(HT1[:].rearrange("p k t -> p (k t)"), pb16(bank), AF.Copy), reads=[PB[bank]], writes=[HT1])

            def proj_tok1(w, c0, n, bank):
                for k in range(8):
                    mm(pf(bank)[:, 0:n], HT1[:, k, :], w[:, k, c0:c0 + n], k == 0, k == 7, [HT1, w], [PB[bank]])

            def proj_fm1(w, c0, m, bank, slot):
                for k in range(8):
                    mm(pf(bank)[0:m, slot * 128:(slot + 1) * 128], w[:, k, c0:c0 + m], HT1[:, k, :], (k == 0 and slot == 0), k == 7,
                       [HT1, w], [PB[bank]], skip_group_check=True)

            S.dma('sp', norm1[:], norm1_d[:, :], writes=[norm1])
            S.dma('sp', fn_bc[:], fnorm_d.partition_broadcast(128), writes=[fn_bc])
            for k in range(8):
                eng = 'dve' if k % 2 == 0 else 'pool'
                S.op(eng, lambda e, k=k: e.tensor_scalar(w_in1[:, k, :], w_in1[:, k, :], norm1[:, k:k + 1], None, ALU.mult),
                     reads=[w_in1, norm1], writes=[w_in1])
            S.op('dve', lambda e: e.memset(VB_lo[:, :, :, 64:65], 1.0), writes=[VB_lo])
            S.op('dve', lambda e: e.memset(VB_hi[:, :, :, 64:65], 1.0), writes=[VB_hi])

            def KT(c):
                return (KT_lo, c) if c < 4 else (KT_hi, c - 4)

            def VB(t):
                return (VB_lo, t) if t < 8 else (VB_hi, t - 8)

            for t in range(ntiles):
                S.dma('sp', X1t[:], x1_d[t * 128:(t + 1) * 128, :], reads=[X1R[t]], writes=[X1t])
                norm_transpose1(0)
                for c in range(8): proj_fm1(w_in1, 1024 + c * 128, 128, 1 + c // 4, c % 4)
                for half in range(2):
                    kt_, _ = KT(half * 4)
                    src = pf(1 + half).rearrange("p (c t) -> p c t", c=4)
                    S.op('act', lambda e, kt_=kt_, src=src, t=t: e.activation(kt_[:, :, t * 128:(t + 1) * 128], src, AF.Copy), reads=[PB[1 + half]], writes=[kt_])
                    S.op('dve', lambda e, half=half, src=src, t=t: e.tensor_reduce(ksum2[:, t, half * 4:(half + 1) * 4], src, AX.X, ALU.add),
                         reads=[PB[1 + half]], writes=[ksum2])
                for half in range(2):
                    proj_tok1(w_in1, 2048 + half * 512, 512, 3 + half)
                vt, ti = VB(t)
                S.op('act', lambda e, vt=vt, ti=ti: e.activation(vt[:, ti, :, 0:64], pf(3, 2).rearrange("p (h d) -> p h d", h=16), AF.Copy),
                     reads=[PB[3], PB[4]], writes=[vt])
            k2 = ksum2[:].rearrange("p (b two) c -> p c b two", two=2)
            S.op('dve', lambda e: e.tensor_tensor(kmean[:], k2[:, :, :, 0], k2[:, :, :, 1], ALU.add), reads=[ksum2], writes=[kmean])
            S.op('dve', lambda e: e.tensor_scalar(kmean[:], kmean[:], 1.0 / 256, None, ALU.mult), reads=[kmean], writes=[kmean])
            S.mark('l1_phaseB')

            STB = [(3, 4), (1, 2)]
            gcount = [0]
            for t in range(ntiles):
                b = t // 2
                causal_only = b < 4
                S.dma('sp', X1t[:], x1_d[t * 128:(t + 1) * 128, :], reads=[X1R[t]], writes=[X1t])
                norm_transpose1(0)
                for c in range(8): proj_fm1(w_in1, c * 128, 128, 1 + c // 4, c % 4)
                for half in range(2):
                    dstb = qT[:, half * 4:(half + 1) * 4, :].rearrange("p c t -> p (c t)")
                    S.op('act', lambda e, half=half, dstb=dstb: e.mul(dstb, pf(1 + half), 0.125), reads=[PB[1 + half]], writes=[qT])
                    if not causal_only:
                        dstf = qTf[:, half * 4:(half + 1) * 4, :].rearrange("p c t -> p (c t)")
                        S.op('dve', lambda e, half=half, dstf=dstf: e.tensor_scalar(dstf, pf(1 + half), 0.125, None, ALU.mult), reads=[PB[1 + half]], writes=[qTf])
                for half in range(2):
                    proj_tok1(w_in1, 3072 + half * 512, 512, 3 + half)
                S.op('act', lambda e: e.activation(ZG1[:], pf(3, 2), AF.Silu), reads=[PB[3], PB[4]], writes=[ZG1])
                if not causal_only:
                    for c in range(8):
                        for hh in range(2):
                            mm(pf(1 + hh)[:, c * 8:(c + 1) * 8], qTf[hh * 64:(hh + 1) * 64, c, :], kmean[hh * 64:(hh + 1) * 64, c, :],
                               c == 0, True, [qTf, kmean], [PB[1 + hh]], skip_group_check=True)
                    gm4 = gm[:].rearrange("p (c two) n -> p c two n", two=2)
                    S.op('dve', lambda e: e.memset(gm[:], NEG), writes=[gm])
                    for hh in range(2):
                        S.op('dve', lambda e, hh=hh, b=b: e.tensor_copy(gm4[:, :, hh, 0:b], pf(1 + hh)[:, 0:64].rearrange("p (c n) -> p c n", c=8)[:, :, 0:b]),
                             reads=[PB[1 + hh]], writes=[gm])
                    for h in range(16):
                        S.op('dve', lambda e, h=h: e.max(top8[:, h, :], gm[:, h, :]), reads=[gm], writes=[top8])
                    S.op('dve', lambda e: e.tensor_tensor(sel[:], gm[:], top8[:, :, 2:3].to_broadcast([128, 16, 8]), ALU.is_ge), reads=[gm, top8], writes=[sel])
                for c in range(8):
                    ktens, ci = KT(c)
                    fb = {5: True, 6: True, 7: True}
                    for g0 in range(0, t + 1, 4):
                        grp = list(range(g0, min(g0 + 4, t + 1))); ns = len(grp)
                        sb = STB[gcount[0] % 2]; pt = PT[gcount[0] % 2]; gcount[0] += 1
                        for hh in range(2):
                            for s_, kt in enumerate(grp):
                                o = pf(sb[hh])[:, s_ * 128:(s_ + 1) * 128]
                                mm(o, ktens[hh * 64:(hh + 1) * 64, ci, kt * 128:(kt + 1) * 128], qT[hh * 64:(hh + 1) * 64, c, :], s_ == 0, kt != t,
                                   [ktens, qT], [PB[sb[hh]]], skip_group_check=True)
                                if kt == t:
                                    mm(o, ident_b, mneg_b, False, True, [cstb], [PB[sb[hh]]], skip_group_check=True)
                        for hh in range(2):
                            S.op('act', lambda e, hh=hh, pt=pt, sb=sb, ns=ns: e.activation(pt[:, hh, 0:ns * 128], pf(sb[hh])[:, 0:ns * 128], AF.Exp),
                                 reads=[PB[sb[hh]]], writes=[pt])
                        for hh in range(2):
                            h = 2 * c + hh
                            for s_, kt in enumerate(grp):
                                n = kt // 2
                                vt, ti = VB(kt)
                                if causal_only or n == b:
                                    bank = 7; o = pf(7)[:, hh * 65:(hh + 1) * 65]
                                else:
                                    bank = 5 + hh; o = pf(bank)[:, n * 65:(n + 1) * 65]
                                mm(o, pt[:, hh, s_ * 128:(s_ + 1) * 128], vt[:, ti, h, :], fb[bank], True, [pt, vt], [PB[bank]], skip_group_check=True)
                                fb[bank] = False
                    for hh in range(2):
                        h = 2 * c + hh
                        S.op('dve', lambda e, h=h, hh=hh: e.tensor_copy(acc[:, h, :], pf(7)[:, hh * 65:(hh + 1) * 65]), reads=[PB[7]], writes=[accR[c]])
                        if not causal_only:
                            for n in range(b):
                                S.op('dve', lambda e, h=h, hh=hh, n=n: e.scalar_tensor_tensor(acc[:, h, :], pf(5 + hh)[:, n * 65:(n + 1) * 65], sel[:, h, n:n + 1],
                                                                                                acc[:, h, :], ALU.mult, ALU.add),
                                     reads=[PB[5 + hh], sel, accR[c]], writes=[accR[c]])
                S.op('dve', lambda e: e.reciprocal(rden[:], acc[:, :, 64]), reads=accR, writes=[rden])
                S.op('dve', lambda e: e.tensor_tensor(F11[:].rearrange("p (h d) -> p h d", h=16), acc[:, :, 0:64],
                                                      rden[:].unsqueeze(2).to_broadcast([128, 16, 64]), ALU.mult), reads=accR + [rden], writes=[F11])
                S.op('pool', lambda e: e.tensor_tensor(mixo[:], F11[:], ZG1[:], ALU.mult), reads=[F11, ZG1], writes=[mixo])
                for k in range(8):
                    S.op('pe', lambda e, k=k: e.transpose(pb16(0)[:, k * 128:(k + 1) * 128], mixo[:, k * 128:(k + 1) * 128], ident_b),
                         reads=[mixo, cstb], writes=[PB[0]])
                S.op('act', lambda e: e.activation(mixoT[:].rearrange("p k t -> p (k t)"), pb16(0), AF.Copy), reads=[PB[0]], writes=[mixoT])
                for n in range(2):
                    for k in range(8):
                        mm(pf(5 + n), mixoT[:, k, :], w_out1[:, k, n * 512:(n + 1) * 512], k == 0, k == 7, [mixoT, w_out1], [PB[5 + n]])
                S.op('dve', lambda e: e.tensor_tensor(X1t[:], X1t[:], pf(5, 2), ALU.add), reads=[X1t, PB[5], PB[6]], writes=[X1t])
                rms_rstd(X1t[:], X1t, ssv, rsv, D, F11[:], F11)
                S.op('dve', lambda e: e.scalar_tensor_tensor(F11[:], X1t[:], rsv.ap, fn_bc[:], ALU.mult, ALU.mult), reads=[X1t, rsv, fn_bc], writes=[F11])
                last = S.dma('sp', out_d[t * 128:(t + 1) * 128, :], F11[:], reads=[F11])
        S.barrier()
        S.emit()
        build.marks = dict(S.marks); build.nops = S.nops
    return nc


_CONST = None


def _consts():
    global _CONST
    if _CONST is None:
        i = np.arange(128)
        ident = np.eye(128, dtype=np.float32)
        tri = (i[:, None] <= i[None, :]).astype(np.float32)
        ones = np.ones((128, 128), np.float32)
        mneg = np.where(i[None, :] < i[:, None], NEG, 0.0).astype(np.float32)
        mbd = ((i[:, None] // 64 == i[None, :] // 64) & (i[None, :] >= i[:, None])).astype(np.float32)
        cst = np.concatenate([ident, tri, ones, mneg, mbd], axis=1)
        rmask = np.ones((128, 512), np.float32); rmask[:, ::64] = 0.0
        _CONST = (np.ascontiguousarray(cst), rmask)
    return _CONST


def _fm(v, k):
    return np.ascontiguousarray(np.asarray(v, np.float32).reshape(k, 128).T)


def make_in_maps(inp):
    cst, rmask = _consts()
    shared = {
        "w_in0": np.ascontiguousarray(inp["even_w_in"][0]),
        "w_out0": np.ascontiguousarray(inp["even_w_out"][0]),
        "w_in1": np.ascontiguousarray(inp["odd_w_in"][0]),
        "w_out1": np.ascontiguousarray(inp["odd_w_out"][0]),
        "norm0": _fm(inp["even_norm"][0], 8),
        "wo0s": np.ascontiguousarray(np.concatenate([_fm(inp["even_ssd_norm"][0], 8), np.tile(inp["even_gla_norm"][0][:, None], (1, 8))], axis=1).astype(np.float32)),
        "norm1": _fm(inp["odd_norm"][0], 8),
        "fnorm": np.ascontiguousarray(inp["final_norm"].reshape(1, D)),
        "convw": np.ascontiguousarray(inp["even_conv_w"][0].reshape(4, 10, 128).transpose(2, 1, 0).reshape(128, 40)),
        "convb": np.ascontiguousarray(inp["even_conv_b"][0].reshape(1, 1280)),
        "hvec": np.ascontiguousarray(np.concatenate([inp["even_a_log"][0], inp["even_dt_bias"][0], inp["even_d_skip"][0]]).reshape(1, 48)),
        "gw2": np.ascontiguousarray(inp["even_gate_w2"][0]),
        "gb": _fm(inp["even_gate_b"][0], 4),
        "cst": cst, "rmask": rmask,
    }
    maps = []
    for b in range(8):
        m = dict(shared)
        m["x"] = np.ascontiguousarray(inp["x"][b])
        maps.append(m)
    return maps


_NC = {}


def kernel(**inputs):
    inp = {k: np.asarray(v) for k, v in inputs.items()}
    stage = "full"
    if stage not in _NC:
        _NC[stage] = build(stage)
    res = run_bass_kernel_spmd(_NC[stage], make_in_maps(inp), core_ids=list(range(8)))
    return np.stack([r["out"] for r in res.results], axis=0).astype(np.float32)
```

```python
import numpy as np
from contextlib import ExitStack
import concourse.bass as bass
import concourse.mybir as mybir
from concourse.bass_utils import run_bass_kernel_spmd

F32 = mybir.dt.float32
BF16 = mybir.dt.bfloat16
AF = mybir.ActivationFunctionType
ALU = mybir.AluOpType
AX = mybir.AxisListType

L = 2048
D = 1024
NT = L // 128
EIN = 5408
C_Z, C_XBC, C_DT, C_Q, C_K, C_V, C_GL, C_G = 0, 1024, 2304, 2320, 2832, 3344, 4368, 4384
EPS = 1e-6
NEG = -1.0e30


class Res:
    __slots__ = ('name', 'lw', 'rd', 'excl')

    def __init__(self, name, excl=False):
        self.name = name; self.lw = None; self.rd = {}; self.excl = excl


class Tok:
    __slots__ = ('key', 'sem', 'val', 'eng')

    def __init__(self, key, sem, val, eng):
        self.key = key; self.sem = sem; self.val = val; self.eng = eng


class Sched:
    ENG = ('pe', 'act', 'dve', 'pool', 'sp')

    def __init__(self, nc, es, n_dma=12, limit=None):
        self.nc = nc
        self.limit = limit; self.nops = 0; self.marks = {}
        self.pending = []; self.reorder = True; self.window = 900; self.est_time = 0.0
        self.prog = {e: [] for e in self.ENG}
        self.cnt = {e: 0 for e in self.ENG}
        self.sem = {e: es.enter_context(nc.semaphore(f"s_{e}")) for e in self.ENG}
        self.seen = {e: {} for e in self.ENG}
        self.nd = n_dma
        self.dsem = {q: [es.enter_context(nc.semaphore(f"d_{q}{i}")) for i in range(n_dma)] for q in ('sp', 'pool', 'act')}
        self.duse = {q: [0] * n_dma for q in self.dsem}
        self.dn = {q: 0 for q in self.dsem}

    def _waits(self, e, reads, writes):
        deps = []
        for r in reads:
            if r.lw is not None: deps.append((r.lw, 'raw'))
        for w in writes:
            if w.lw is not None: deps.append((w.lw, 'waw'))
            for t in w.rd.values(): deps.append((t, 'war'))
        need = {}
        for tok, kind in deps:
            if tok.eng == e and e == 'pe':
                continue
            if self.seen[e].get(tok.key, 0) >= tok.val: continue
            if tok.key not in need or need[tok.key].val < tok.val:
                need[tok.key] = tok
        for k, t in need.items():
            self.seen[e][k] = t.val
        return list(need.values())

    def _commit(self, tok, reads, writes):
        for r in reads:
            o = r.rd.get(tok.key)
            if o is None or o.val < tok.val: r.rd[tok.key] = tok
        for w in writes:
            w.lw = tok; w.rd = {}

    def mark(self, name):
        self.marks[name] = self.nops

    def op(self, e, fn, reads=(), writes=()):
        self.nops += 1
        if self.limit is not None and self.nops > self.limit: return None
        self.pending.append(('op', e, fn, list(reads), list(writes), None))

    def dma(self, q, out, in_, reads=(), writes=(), **kw):
        self.nops += 1
        if self.limit is not None and self.nops > self.limit: return None
        self.pending.append(('dma', q, (out, in_, kw), list(reads), list(writes), None))

    @staticmethod
    def _norm_rw(reads, writes):
        reads = [getattr(r, 'r', r) for r in reads]
        writes = [getattr(w, 'r', w) for w in writes]
        writes = writes + [r for r in reads if r.excl and r not in writes]
        reads = [r for r in reads if not r.excl]
        return reads, writes

    def _cost(self, item):
        kind, e, payload, _, _, _ = item
        if kind == 'dma':
            out, in_, kw = payload
            nbytes = 4
            try:
                nbytes = int(np.prod(out.shape)) * 4
            except Exception:
                pass
            return (1500.0 if e == 'pool' else 80.0), 2500.0 + nbytes * 0.012, None
        pr = _Probe()
        try:
            payload(pr)
        except Exception:
            return 300.0, 0.0, None
        name, a, k = pr.calls[0] if pr.calls else ('?', (), {})

        def fsz(ap):
            try:
                return int(np.prod(ap.shape[1:]))
            except Exception:
                return 64
        aset = None
        if e == 'pe':
            if name == 'matmul':
                rhs = a[2] if len(a) > 2 else k.get('rhs')
                N = max(64, fsz(rhs))
                c = 30.0 + N * 0.42
                try:
                    if rhs.dtype == F32: c *= 4.0
                except Exception:
                    pass
            else:
                c = 70.0
        else:
            out = a[0] if a else k.get('out')
            n = fsz(out)
            if e == 'dve':
                c = 70.0 + n * 1.05
                if name == 'tensor_tensor_scan': c += n * 1.05
            elif e == 'act':
                c = 200.0 + n * 0.84
                if name == 'activation':
                    f = a[2] if len(a) > 2 else k.get('func')
                    aset = 'silu' if f == AF.Silu else ('any' if f in (AF.Copy, AF.Square, AF.Identity) else 'exp')
                else:
                    aset = 'any'
            else:
                c = 260.0 + n * 1.1
        return c, 0.0, aset

    def flush(self):
        ops = self.pending; self.pending = []
        if not ops: return
        n = len(ops)
        if not self.reorder:
            order = range(n)
        else:
            order = self._schedule(ops)
        for i in order:
            kind, e, payload, R, W, _ = ops[i]
            if kind == 'op':
                self._op(e, payload, R, W)
            else:
                out, in_, kw = payload
                self._dma(e, out, in_, R, W, **kw)

    def _schedule(self, ops):
        import heapq
        n = len(ops)
        lastw = {}; readers = {}
        preds = [None] * n; succs = [[] for _ in range(n)]; indeg = [0] * n
        for i, it in enumerate(ops):
            R, W = self._norm_rw(it[3], it[4])
            p = set()
            for r in R:
                j = lastw.get(id(r))
                if j is not None: p.add(j)
            for w in W:
                j = lastw.get(id(w))
                if j is not None: p.add(j)
                p |= readers.get(id(w), set())
            p.discard(i)
            for r in R: readers.setdefault(id(r), set()).add(i)
            for w in W:
                lastw[id(w)] = i; readers[id(w)] = set()
            preds[i] = p
            indeg[i] = len(p)
            for j in p: succs[j].append(i)
        cost = [self._cost(it) for it in ops]
        eng = [it[1] for it in ops]
        ready_t = [0.0] * n
        fin = [0.0] * n
        efree = {e: 0.0 for e in self.ENG}
        wait_h = {e: [] for e in self.ENG}
        pool_h = {e: [] for e in self.ENG}
        aset_cur = [None]
        for i in range(n):
            if indeg[i] == 0: heapq.heappush(wait_h[eng[i]], (0.0, i))
        order = []
        WINDOW = self.window
        done = 0
        lo = 0
        sched = [False] * n
        while done < n:
            best = None
            for e in self.ENG:
                wh, ph = wait_h[e], pool_h[e]
                while wh and wh[0][0] <= efree[e]:
                    rt, i = heapq.heappop(wh); heapq.heappush(ph, i)
                cand = None
                if ph:
                    i = ph[0]
                    if i - lo <= WINDOW:
                        cand = (efree[e], i)
                if cand is None and wh:
                    rt, i = wh[0]
                    if i - lo <= WINDOW:
                        cand = (max(rt, efree[e]), i)
                if cand is not None and (best is None or cand < best[0]):
                    best = (cand, e)
            if best is None:
                cands = []
                for e in self.ENG:
                    if pool_h[e]: cands.append((pool_h[e][0], e, 'p'))
                    if wait_h[e]: cands.append((min(x[1] for x in wait_h[e]), e, 'w'))
                i, e, src = min(cands)
                if src == 'p':
                    pool_h[e].remove(i); heapq.heapify(pool_h[e])
                else:
                    wait_h[e] = [x for x in wait_h[e] if x[1] != i]; heapq.heapify(wait_h[e])
                start = max(ready_t[i], efree[e])
            else:
                (start, i), e = best
                if pool_h[e] and pool_h[e][0] == i:
                    heapq.heappop(pool_h[e])
                else:
                    heapq.heappop(wait_h[e])
            c, lat, aset = cost[i]
            if e == 'act' and aset not in (None, 'any'):
                if aset_cur[0] is not None and aset_cur[0] != aset: c += 1300.0
                aset_cur[0] = aset
            efree[e] = start + c
            fin[i] = start + c + lat
            sched[i] = True; done += 1; order.append(i)
            while lo < n and sched[lo]: lo += 1
            for j in succs[i]:
                t = fin[i] + (0.0 if eng[j] == e and lat == 0.0 else 120.0)
                if t > ready_t[j]: ready_t[j] = t
                indeg[j] -= 1
                if indeg[j] == 0:
                    heapq.heappush(wait_h[eng[j]], (ready_t[j], j))
        self.est_time = max(fin) if fin else 0.0
        return order

    def _op(self, e, fn, reads=(), writes=()):
        reads = [getattr(r, 'r', r) for r in reads]
        writes = [getattr(w, 'r', w) for w in writes]
        writes = writes + [r for r in reads if r.excl and r not in writes]
        reads = [r for r in reads if not r.excl]
        waits = self._waits(e, reads, writes)
        self.cnt[e] += 1
        tok = Tok(e, self.sem[e], self.cnt[e], e)
        self.prog[e].append((waits, fn, self.sem[e], 1))
        self._commit(tok, reads, writes)
        return tok

    def _dma(self, q, out, in_, reads=(), writes=(), **kw):
        reads = [getattr(r, 'r', r) for r in reads]
        writes = [getattr(w, 'r', w) for w in writes]
        waits = self._waits(q, reads, writes)
        n = self.dn[q]; self.dn[q] += 1
        slot = n % self.nd
        sem = self.dsem[q][slot]
        self.duse[q][slot] += 1
        val = 16 * self.duse[q][slot]
        key = f"d_{q}{slot}"
        if val > 16 and self.seen[q].get(key, 0) < val - 16:
            waits.append(Tok(key, sem, val - 16, None))
            self.seen[q][key] = val - 16
        tok = Tok(key, sem, val, None)
        self.prog[q].append((waits, lambda eng: eng.dma_start(out=out, in_=in_, **kw), sem, 16))
        self._commit(tok, reads, writes)
        return tok

    def wait_tok(self, e, tok):
        if tok is None: return
        if self.seen[e].get(tok.key, 0) >= tok.val: return
        self.seen[e][tok.key] = tok.val
        self.prog[e].append(([tok], None, None, 0))

    def barrier(self):
        self.flush()
        if self.limit is not None and self.nops > self.limit: return
        toks = []
        for e in ('pe', 'act', 'dve', 'pool'):
            if self.cnt[e] > 0: toks.append(Tok(e, self.sem[e], self.cnt[e], e))
        for q in self.dsem:
            for i in range(self.nd):
                if self.duse[q][i] > 0: toks.append(Tok(f"d_{q}{i}", self.dsem[q][i], 16 * self.duse[q][i], None))
        for e in self.ENG:
            for t in toks:
                self.wait_tok(e, t)

    def emit(self):
        self.flush()
        nc = self.nc
        needed = set()
        for e in self.ENG:
            for waits, fn, sem, inc in self.prog[e]:
                for w in waits:
                    if w.eng is not None:
                        needed.add((w.eng, w.val))
        rank = {}
        for e in self.ENG:
            r = 0; seq = 0
            for waits, fn, sem, inc in self.prog[e]:
                if fn is not None and inc == 1:
                    seq += 1
                    if (e, seq) in needed:
                        r += 1
                        rank[(e, seq)] = r
        with nc.Block() as block:
            def run(name):
                def body(eng):
                    seq = 0
                    for waits, fn, sem, inc in self.prog[name]:
                        for w in waits:
                            v = rank[(w.eng, w.val)] if w.eng is not None else w.val
                            eng.wait_ge(w.sem, v)
                        if fn is not None:
                            ins = fn(eng)
                            if inc == 1:
                                seq += 1
                                if (name, seq) in rank:
                                    ins.then_inc(sem, 1)
                            else:
                                ins.then_inc(sem, inc)
                return body
            block.tensor(run('pe'))
            block.scalar(run('act'))
            block.vector(run('dve'))
            block.gpsimd(run('pool'))
            block.sync(run('sp'))


class _Probe:
    def __init__(self):
        self.calls = []

    def __getattr__(self, name):
        def f(*a, **k):
            self.calls.append((name, a, k))
            return None
        return f


class T:
    def __init__(self, nc, es, name, shape, dt):
        self.t = es.enter_context(nc.sbuf_tensor("sb_" + name, list(shape), dt))
        self.r = Res(name)

    def __getitem__(self, k):
        return self.t[k]

    @staticmethod
    def view(ap, res):
        o = T.__new__(T); o.t = ap; o.r = res
        return o


def build(stage="full", ntiles=NT, limit=None):
    nc = bass.Bass("TRN2", target_bir_lowering=False)
    dram_in = lambda name, shape: nc.dram_tensor(name, list(shape), F32, kind="ExternalInput").ap()
    x_in = dram_in("x", [L, D])
    w_in0_d = dram_in("w_in0", [D, EIN])
    w_out0_d = dram_in("w_out0", [2 * D, D])
    w_in1_d = dram_in("w_in1", [D, 4 * D])
    w_out1_d = dram_in("w_out1", [D, D])
    norm0_d = dram_in("norm0", [128, 8])
    wo0s_d = dram_in("wo0s", [128, 16])
    norm1_d = dram_in("norm1", [128, 8])
    fnorm_d = dram_in("fnorm", [1, D])
    convw_d = dram_in("convw", [128, 40])
    convb_d = dram_in("convb", [1, 1280])
    hvec_d = dram_in("hvec", [1, 48])
    gw2_d = dram_in("gw2", [16, 512])
    gb_d = dram_in("gb", [128, 4])
    cst_d = dram_in("cst", [128, 5 * 128])
    rmask_d = dram_in("rmask", [128, 512])
    out_d = nc.dram_tensor("out", [L, D], F32, kind="ExternalOutput").ap()
    x1_d = nc.dram_tensor("x1s", [L, D], F32, kind="Internal").ap()

    es = ExitStack()
    with es:
        S = Sched(nc, es, limit=limit)
        mkp = lambda name, shape, dt: T(nc, es, name, shape, dt)
        ARENA_BYTES = 86 * 1024
        arena = es.enter_context(nc.sbuf_tensor("arena", [128, ARENA_BYTES // 4], F32))
        aoff = [0]

        def mk(name, shape, dt):
            n = int(np.prod(shape[1:])); esz = 4 if dt == F32 else 2
            nb = ((n * esz + 31) // 32) * 32
            f0 = aoff[0] // 4; aoff[0] += nb
            assert aoff[0] <= ARENA_BYTES, ("arena overflow", name, aoff[0])
            v = arena[0:shape[0], f0:f0 + nb // 4]
            if dt == BF16: v = v.bitcast(BF16)
            v = v[:, 0:n]
            if len(shape) == 3: v = v.rearrange("p (a b) -> p a b", a=shape[1])
            if len(shape) == 4: v = v.rearrange("p (a b c) -> p a b c", a=shape[1], b=shape[2])
            return T.view(v, Res(name))
        pst = es.enter_context(nc.psum_tensor("pst", [128, 8, 512], F32))
        PB = [Res(f"bank{i}", excl=True) for i in range(8)]

        def pf(b, n=1):
            return pst[:, b:b + n, :].rearrange("p b f -> p (b f)")

        def pb16(b):
            return pst[:, b, :].bitcast(BF16)

        cst = mkp("cst", [128, 640], F32)
        S.dma('sp', cst[:], cst_d[:, :], writes=[cst])
        ident_f = cst[:, 0:128]; tri_f = cst[:, 128:256]; ones_f = cst[:, 256:384]
        cstb = mkp("cstb", [128, 640], BF16)
        S.op('dve', lambda e: e.tensor_copy(cstb[:], cst[:]), reads=[cst], writes=[cstb])
        ident_b = cstb[:, 0:128]; ones_b = cstb[:, 256:384]; mneg_b = cstb[:, 384:512]; mbd_b = cstb[:, 512:640]
        rmask = mk("rmask", [128, 512], BF16)
        S.dma('pool', rmask[:], rmask_d[:, :], writes=[rmask])

        w_in0 = mkp("w_in0", [128, 8, EIN], BF16)
        w_out0 = mkp("w_out0", [128, 16, D], BF16)
        w0v = w_in0_d.rearrange("(k p) n -> p k n", p=128)
        col_chunks = [(c, min(c + 512, EIN)) for c in range(0, EIN, 512)]
        for (c0, c1) in col_chunks:
            S.dma('pool', w_in0[:, :, c0:c1], w0v[:, :, c0:c1], writes=[w_in0])
        wo0v = w_out0_d.rearrange("(k p) n -> p k n", p=128)
        for k0 in range(0, 16, 4):
            S.dma('pool', w_out0[:, k0:k0 + 4, :], wo0v[:, k0:k0 + 4, :], writes=[w_out0])
        norm0 = mk("norm0", [128, 8], F32)
        S.dma('sp', norm0[:], norm0_d[:, :], writes=[norm0])
        wo0s = mk("wo0s", [128, 16], F32)
        S.dma('sp', wo0s[:], wo0s_d[:, :], writes=[wo0s])
        for k in range(8):
            S.op('dve', lambda e, k=k: e.tensor_scalar(w_in0[:, k, :], w_in0[:, k, :], norm0[:, k:k + 1], None, ALU.mult),
                 reads=[w_in0, norm0], writes=[w_in0])
        for k in range(16):
            S.op('pool', lambda e, k=k: e.tensor_scalar(w_out0[:, k, :], w_out0[:, k, :], wo0s[:, k:k + 1], None, ALU.mult),
                 reads=[w_out0, wo0s], writes=[w_out0])
        convw = mk("convw", [128, 40], F32)
        S.dma('sp', convw[:], convw_d[:, :], writes=[convw])
        convb = mk("convb", [1, 1280], BF16)
        S.dma('pool', convb[:], convb_d[:, :], writes=[convb])
        cdiag = mk("cdiag", [128, 40, 128], BF16)
        for i in range(40):
            S.op('dve', lambda e, i=i: e.tensor_scalar(cdiag[:, i, :], ident_f, convw[:, i:i + 1], None, ALU.mult),
                 reads=[cst, convw], writes=[cdiag])
        hvec = mk("hvec", [128, 48], F32)
        S.dma('sp', hvec[:], hvec_d.partition_broadcast(128), writes=[hvec])
        a_bc = mk("a_bc", [128, 16], F32)
        S.op('act', lambda e: e.activation(a_bc[:], hvec[:, 0:16], AF.Exp), reads=[hvec], writes=[a_bc])
        S.op('dve', lambda e: e.tensor_scalar(a_bc[:], a_bc[:], -1.0, None, ALU.mult), reads=[a_bc], writes=[a_bc])
        dtb_bc = hvec[:, 16:32]
        DI = mk("DI", [128, 16, 128], BF16)
        for h in range(16):
            S.op('dve', lambda e, h=h: e.tensor_scalar(DI[:, h, :], ident_f, hvec[:, 32 + h:33 + h], None, ALU.mult),
                 reads=[cst, hvec], writes=[DI])
        gw2 = mk("gw2", [16, 512], BF16)
        S.dma('pool', gw2[:], gw2_d[:, :], writes=[gw2])
        ngb = mk("ngb", [128, 4], F32)
        S.dma('sp', ngb[:], gb_d[:, :], writes=[ngb])
        S.op('dve', lambda e: e.tensor_scalar(ngb[:], ngb[:], -1.0, None, ALU.mult), reads=[ngb], writes=[ngb])

        P_x = mk("P_x", [128, D], F32)
        P_hb = mk("P_hb", [128, D], BF16)
        P_hT = mk("P_hT", [128, 8, 128], BF16)
        P_zg = mk("P_zg", [128, D], F32)
        P_v = mk("P_v", [128, D], BF16)
        P_U = mk("P_U", [128, 10, 131], BF16)
        P_halo = mk("P_halo", [128, 10, 3], BF16)
        P_xc = mk("P_xc", [128, 10, 128], BF16)
        P_mix = mk("P_mix", [128, 2 * D], BF16)
        P_q = mk("P_q", [128, 512], F32)
        P_k = mk("P_k", [128, 512], F32)
        P_F1 = mk("P_F1", [128, D], F32)
        P_F2 = mk("P_F2", [128, D], F32)
        P_F3 = mk("P_F3", [128, D], F32)
        P_F4 = mk("P_F4", [128, 512], F32)
        P_scm = mk("P_scm", [128, 8, 128], BF16)
        P_qd = mk("P_qd", [128, 4, 128], BF16)
        P_kd = mk("P_kd", [128, 4, 128], BF16)
        P_kl = mk("P_kl", [128, 4, 128], BF16)
        P_qg = mk("P_qg", [128, 4, 128], BF16)
        glr = mk("glr", [16, 128], BF16)
        Btok = mk("Btok", [128, 128], BF16)
        sm = mk("sm", [128, 256], F32)
        sm_r = {}

        def smv(name, c0, n):
            if name not in sm_r: sm_r[name] = Res("sm_" + name)

            class V:
                r = sm_r[name]
                ap = sm[:, c0:c0 + n]
            return V
        ss_ = smv("ss", 0, 1); rstd_ = smv("rstd", 1, 1)
        dtb_ = smv("dtb", 16, 16); dt_ = smv("dt", 32, 16); dA_ = smv("dA", 48, 16)
        nacs_ = smv("nacs", 64, 16); eacs_ = smv("eacs", 80, 16); dd_ = smv("dd", 96, 16)
        wdt_ = smv("wdt", 112, 16); eal_ = smv("eal", 128, 8); ssy_ = smv("ssy", 136, 1)
        rsy_ = smv("rsy", 137, 1); edec_ = smv("edec", 144, 8); sso_ = smv("sso", 152, 8)
        rso_ = smv("rso", 160, 8)
        Sst = mk("Sst", [128, 512], F32); Sbf = mk("Sbf", [128, 512], BF16)
        GS = mk("GS", [128, 4, 128], F32); GSb0 = mk("GSb0", [128, 4, 128], BF16); GSb1 = mk("GSb1", [128, 4, 128], BF16)
        S.op('dve', lambda e: e.memset(Sst[:], 0.0), writes=[Sst])
        S.op('dve', lambda e: e.memset(Sbf[:], 0.0), writes=[Sbf])
        S.op('dve', lambda e: e.memset(GS[:], 0.0), writes=[GS])
        S.op('dve', lambda e: e.memset(GSb0[:], 0.0), writes=[GSb0])
        S.op('dve', lambda e: e.memset(P_halo[:], 0.0), writes=[P_halo])

        def mm(out, lhsT, rhs, start, stop, R, W, **kw):
            S.op('pe', lambda e: e.matmul(out, lhsT, rhs, start=start, stop=stop, **kw), reads=R, writes=W)

        def rms_rstd(src_ap, src_r, ss_v, rs_v, n, scratch_ap, scratch_r):
            S.op('act', lambda e: e.activation(scratch_ap, src_ap, AF.Square, accum_out=ss_v.ap), reads=[src_r], writes=[scratch_r, ss_v])
            S.op('dve', lambda e: e.tensor_scalar(rs_v.ap, ss_v.ap, 1.0 / n, EPS, ALU.mult, ALU.add), reads=[ss_v], writes=[rs_v])
            S.op('act', lambda e: e.activation(rs_v.ap, rs_v.ap, AF.Ln), reads=[rs_v], writes=[rs_v])
            S.op('act', lambda e: e.activation(rs_v.ap, rs_v.ap, AF.Exp, scale=-0.5), reads=[rs_v], writes=[rs_v])

        def norm_transpose(xsrc, bank):
            rms_rstd(xsrc[:], xsrc, ss_, rstd_, D, P_F1[:], P_F1)
            S.op('dve', lambda e: e.tensor_scalar(P_hb[:], xsrc[:], rstd_.ap, None, ALU.mult), reads=[xsrc, rstd_], writes=[P_hb])
            for k in range(8):
                S.op('pe', lambda e, k=k: e.transpose(pb16(bank)[:, k * 128:(k + 1) * 128], P_hb[:, k * 128:(k + 1) * 128], ident_b),
                     reads=[P_hb, cstb], writes=[PB[bank]])
            S.op('act', lambda e: e.activation(P_hT[:].rearrange("p k t -> p (k t)"), pb16(bank), AF.Copy), reads=[PB[bank]], writes=[P_hT])

        def proj_tok(w, c0, n, bank):
            for k in range(8):
                mm(pf(bank)[:, 0:n], P_hT[:, k, :], w[:, k, c0:c0 + n], k == 0, k == 7, [P_hT, w], [PB[bank]])

        def proj_fm(w, c0, m, bank, slot):
            for k in range(8):
                mm(pf(bank)[0:m, slot * 128:(slot + 1) * 128], w[:, k, c0:c0 + m], P_hT[:, k, :], (k == 0 and slot == 0), k == 7,
                   [P_hT, w], [PB[bank]], skip_group_check=True)

        C1 = 1.0 / 16.0
        LN8 = float(np.log(0.125))

        X1R = [Res(f'x1_{t}') for t in range(NT)]
        S.mark('setup_done')

        def layer0_tile(t):
            S.dma('sp', P_x[:], x_in[t * 128:(t + 1) * 128, :], writes=[P_x])
            norm_transpose(P_x, 0)
            S.mark(f'l0_{t}_projfm')
            for s in range(4): proj_fm(w_in0, C_Q + s * 128, 128, 1, s)
            S.op('act', lambda e: e.activation(P_q[:], pf(1), AF.Copy), reads=[PB[1]], writes=[P_q])
            for s in range(4): proj_fm(w_in0, C_K + s * 128, 128, 2, s)
            S.op('dve', lambda e: e.tensor_copy(P_k[:], pf(2)), reads=[PB[2]], writes=[P_k])
            proj_fm(w_in0, C_GL, 16, 3, 0)
            S.op('dve', lambda e: e.tensor_copy(glr[:], pf(3)[0:16, 0:128]), reads=[PB[3]], writes=[glr])
            S.op('dve', lambda e: e.tensor_copy(P_U[:, :, 0:3], P_halo[:]), reads=[P_halo], writes=[P_U])
            for grp, bank in ((0, 1), (1, 2), (2, 3)):
                nch = 4 if grp < 2 else 2
                for s in range(nch): proj_fm(w_in0, C_XBC + (grp * 4 + s) * 128, 128, bank, s)
                eng = 'act' if grp != 1 else 'dve'
                src = pf(bank)[:, 0:nch * 128].rearrange("p (c t) -> p c t", c=nch)
                dst = P_U[:, grp * 4:grp * 4 + nch, 3:131]
                if eng == 'act':
                    S.op('act', lambda e, src=src, dst=dst: e.activation(dst, src, AF.Copy), reads=[PB[bank]], writes=[P_U])
                else:
                    S.op('dve', lambda e, src=src, dst=dst: e.tensor_copy(dst, src), reads=[PB[bank]], writes=[P_U])
            S.op('dve', lambda e: e.tensor_copy(P_halo[:], P_U[:, :, 128:131]), reads=[P_U], writes=[P_halo])
            proj_tok(w_in0, C_DT, 16, 3)
            S.op('dve', lambda e: e.tensor_tensor(dtb_.ap, pf(3)[:, 0:16], dtb_bc, ALU.add), reads=[PB[3], hvec], writes=[dtb_])
            S.mark(f'l0_{t}_glaprep')
            for ch in range(4):
                mm(pf(1)[:, ch * 128:(ch + 1) * 128], gw2[0:16, ch * 128:(ch + 1) * 128], glr[:], ch == 0, True, [gw2, glr], [PB[1]], skip_group_check=True)
            el = P_F2[:, 0:512]; Nn = P_F2[:, 512:1024]
            for ch in range(4):
                S.op('act', lambda e, ch=ch: e.activation(el[:, ch * 128:(ch + 1) * 128], pf(1)[:, ch * 128:(ch + 1) * 128], AF.Exp, scale=-1.0, bias=ngb[:, ch:ch + 1]),
                     reads=[PB[1], ngb], writes=[P_F2])
            S.op('act', lambda e: e.activation(el, el, AF.Ln, bias=1.0), reads=[P_F2], writes=[P_F2])
            S.op('act', lambda e: e.activation(dt_.ap, dtb_.ap, AF.Exp), reads=[dtb_], writes=[dt_])
            S.op('act', lambda e: e.activation(dt_.ap, dt_.ap, AF.Ln, bias=1.0), reads=[dt_], writes=[dt_])
            S.op('dve', lambda e: e.tensor_tensor_scan(Nn, rmask[:], el, 0.0, ALU.mult, ALU.add), reads=[rmask, P_F2], writes=[P_F2])
            N3 = Nn.rearrange("p (s t) -> p s t", t=64)
            Dref = P_F3[:, 0:512]; Dlast = P_F3[:, 512:1024]
            S.op('dve', lambda e: e.tensor_tensor(Dref.rearrange("p (s t) -> p s t", t=64), N3, N3[:, :, 32:33].to_broadcast([128, 8, 64]), ALU.subtract),
                 reads=[P_F2], writes=[P_F3])
            S.op('dve', lambda e: e.tensor_tensor(Dlast.rearrange("p (s t) -> p s t", t=64), N3[:, :, 63:64].to_broadcast([128, 8, 64]), N3, ALU.subtract),
                 reads=[P_F2], writes=[P_F3])
            E2 = P_F4[:]
            S.op('act', lambda e: e.activation(E2, Dref, AF.Exp, scale=C1), reads=[P_F3], writes=[P_F4])
            S.op('act', lambda e: e.activation(Dref, Dref, AF.Exp, scale=-C1, bias=LN8), reads=[P_F3], writes=[P_F3])
            S.op('act', lambda e: e.activation(Dlast, Dlast, AF.Exp, scale=-C1), reads=[P_F3], writes=[P_F3])
            S.op('act', lambda e: e.activation(edec_.ap, N3[:, :, 63], AF.Exp, scale=-C1), reads=[P_F2], writes=[edec_])
            S.op('act', lambda e: e.activation(Nn, Nn, AF.Exp, scale=-C1, bias=LN8), reads=[P_F2], writes=[P_F2])
            f4 = lambda ap: ap.rearrange("p c t -> p (c t)")
            S.op('dve', lambda e: e.tensor_tensor(f4(P_qd[:]), P_q[:], Dref, ALU.mult), reads=[P_q, P_F3], writes=[P_qd])
            S.op('dve', lambda e: e.tensor_tensor(f4(P_kd[:]), P_k[:], E2, ALU.mult), reads=[P_k, P_F4], writes=[P_kd])
            S.op('dve', lambda e: e.tensor_tensor(f4(P_kl[:]), P_k[:], Dlast, ALU.mult), reads=[P_k, P_F3], writes=[P_kl])
            S.op('dve', lambda e: e.tensor_tensor(f4(P_qg[:]), P_q[:], Nn, ALU.mult), reads=[P_q, P_F2], writes=[P_qg])
            S.mark(f'l0_{t}_conv')
            for grp, bank in ((0, 1), (1, 2), (2, 3)):
                nch = 4 if grp < 2 else 2
                for s in range(nch):
                    c = grp * 4 + s
                    o = pf(bank)[:, s * 128:(s + 1) * 128]
                    for k in range(4):
                        mm(o, cdiag[:, c * 4 + k, :], P_U[:, c, k:k + 128], (k == 0 and s == 0), False, [cdiag, P_U], [PB[bank]], skip_group_check=True)
                    mm(o, convb[0:1, c * 128:(c + 1) * 128], ones_b[0:1, :], False, True, [convb, cstb], [PB[bank]], skip_group_check=True)
                S.op('act', lambda e, grp=grp, nch=nch, bank=bank: e.activation(
                    P_xc[:, grp * 4:grp * 4 + nch, :].rearrange("p c t -> p (c t)"), pf(bank)[:, 0:nch * 128], AF.Silu),
                    reads=[PB[bank]], writes=[P_xc])
            for half in range(2):
                proj_tok(w_in0, C_Z + half * 512, 512, 4 + half)
            S.op('act', lambda e: e.activation(P_zg[:], pf(4, 2), AF.Silu), reads=[PB[4], PB[5]], writes=[P_zg])
            S.mark(f'l0_{t}_ssd')
            xs_tok = P_F4[:].bitcast(BF16)
            for c in range(8):
                S.op('pe', lambda e, c=c: e.transpose(pb16(0)[:, c * 128:(c + 1) * 128], P_xc[:, c, :], ident_b), reads=[P_xc, cstb], writes=[PB[0]])
            S.op('dve', lambda e: e.tensor_copy(xs_tok, pb16(0)), reads=[PB[0]], writes=[P_F4])
            S.op('pe', lambda e: e.transpose(pb16(0)[:, 0:128], P_xc[:, 8, :], ident_b), reads=[P_xc, cstb], writes=[PB[0]])
            S.op('act', lambda e: e.activation(Btok[:], pb16(0)[:, 0:128], AF.Copy), reads=[PB[0]], writes=[Btok])
            S.op('dve', lambda e: e.tensor_tensor(dA_.ap, dt_.ap, a_bc[:], ALU.mult), reads=[dt_, a_bc], writes=[dA_])
            mm(pf(1)[:, 0:16], tri_f, dA_.ap, True, True, [cst, dA_], [PB[1]], skip_group_check=True)
            mm(pf(1)[:, 16:32], ones_f, dA_.ap, False, True, [cst, dA_], [PB[1]], skip_group_check=True)
            S.op('dve', lambda e: e.tensor_scalar(nacs_.ap, pf(1)[:, 0:16], -1.0, None, ALU.mult), reads=[PB[1]], writes=[nacs_])
            S.op('act', lambda e: e.activation(eacs_.ap, pf(1)[:, 0:16], AF.Exp), reads=[PB[1]], writes=[eacs_])
            S.op('dve', lambda e: e.tensor_tensor(dd_.ap, pf(1)[:, 16:32], nacs_.ap, ALU.add), reads=[PB[1], nacs_], writes=[dd_])
            S.op('act', lambda e: e.activation(dd_.ap, dd_.ap, AF.Exp), reads=[dd_], writes=[dd_])
            S.op('dve', lambda e: e.tensor_tensor(wdt_.ap, dd_.ap, dt_.ap, ALU.mult), reads=[dd_, dt_], writes=[wdt_])
            S.op('act', lambda e: e.activation(eal_.ap[0:64, :], pf(1)[0:64, 16:24], AF.Exp), reads=[PB[1]], writes=[eal_])
            S.op('act', lambda e: e.activation(eal_.ap[64:128, :], pf(1)[64:128, 24:32], AF.Exp), reads=[PB[1]], writes=[eal_])
            xdt = P_q[:].bitcast(BF16); xw = P_k[:].bitcast(BF16)
            h3 = lambda ap: ap.rearrange("p (h d) -> p h d", h=16)
            S.op('dve', lambda e: e.tensor_tensor(h3(xdt), h3(xs_tok), dt_.ap.unsqueeze(2).to_broadcast([128, 16, 64]), ALU.mult),
                 reads=[P_F4, dt_, P_qd, P_qg], writes=[P_q])
            S.op('dve', lambda e: e.tensor_tensor(h3(xw), h3(xs_tok), wdt_.ap.unsqueeze(2).to_broadcast([128, 16, 64]), ALU.mult),
                 reads=[P_F4, wdt_, P_kd, P_kl], writes=[P_k])
            for g in range(2):
                mm(pf(2 + g)[:, 0:128], P_xc[g * 64:(g + 1) * 64, 8, :], P_xc[g * 64:(g + 1) * 64, 9, :], True, True,
                   [P_xc], [PB[2 + g]])
            S.mark(f'l0_{t}_segsum')
            EG = P_F3[:].bitcast(BF16).rearrange("p (h i) -> p h i", h=16)
            for h in range(16):
                bank = 4 + h // 4
                o = pf(bank)[:, (h % 4) * 128:(h % 4 + 1) * 128]
                mm(o, dA_.ap[:, h:h + 1].to_broadcast([128, 128]), tri_f, h % 4 == 0, False, [dA_, cst], [PB[bank]], skip_group_check=True)
                mm(o, ident_b, mneg_b, False, True, [cstb], [PB[bank]], skip_group_check=True)
                S.op('act', lambda e, h=h, o=o: e.activation(EG[:, h, :], o, AF.Exp, bias=nacs_.ap[:, h:h + 1]),
                     reads=[PB[bank], nacs_, P_qd, P_kl], writes=[P_F3])
            for g in range(2):
                S.op('dve', lambda e, g=g: e.tensor_tensor(EG[:, g * 8:(g + 1) * 8, :], EG[:, g * 8:(g + 1) * 8, :],
                                                           pf(2 + g)[:, 0:128].unsqueeze(1).to_broadcast([128, 8, 128]), ALU.mult),
                     reads=[P_F3, PB[2 + g]], writes=[P_F3])
            S.mark(f'l0_{t}_ydiag')
            for h in range(16):
                bank = 4 + h // 8
                o = pf(bank)[:, (h % 8) * 64:(h % 8 + 1) * 64]
                mm(o, EG[:, h, :], xdt[:, h * 64:(h + 1) * 64], h % 8 == 0, False, [P_F3, P_q], [PB[bank]], skip_group_check=True)
                mm(o, DI[:, h, :], xs_tok[:, h * 64:(h + 1) * 64], False, True, [DI, P_F4], [PB[bank]], skip_group_check=True)
            for g in range(2):
                mm(pf(6 + g), P_xc[g * 64:(g + 1) * 64, 9, :], Sbf[g * 64:(g + 1) * 64, :], True, True, [P_xc, Sbf], [PB[6 + g]])
            S.op('dve', lambda e: e.tensor_tensor(h3(P_F1[:]), h3(pf(6, 2)), eacs_.ap.unsqueeze(2).to_broadcast([128, 16, 64]), ALU.mult),
                 reads=[PB[6], PB[7], eacs_], writes=[P_F1])
            S.op('dve', lambda e: e.tensor_tensor(P_F1[:], P_F1[:], pf(4, 2), ALU.add), reads=[P_F1, PB[4], PB[5]], writes=[P_F1])
            S.op('pool', lambda e: e.tensor_tensor(P_F1[:], P_F1[:], P_zg[:], ALU.mult), reads=[P_F1, P_zg], writes=[P_F1])
            rms_rstd(P_F1[:], P_F1, ssy_, rsy_, D, P_F2[:], P_F2)
            S.op('dve', lambda e: e.tensor_scalar(P_mix[:, 0:D], P_F1[:], rsy_.ap, None, ALU.mult), reads=[P_F1, rsy_], writes=[P_mix])
            for g in range(2):
                mm(pf(1)[g * 64:(g + 1) * 64, :], Btok[:, g * 64:(g + 1) * 64], xw[:, g * 512:(g + 1) * 512], True, True, [Btok, P_k], [PB[1]], skip_group_check=True)
            S.op('dve', lambda e: e.tensor_tensor(Sst[:].rearrange("p (h d) -> p h d", h=8), Sst[:].rearrange("p (h d) -> p h d", h=8),
                                                  eal_.ap.unsqueeze(2).to_broadcast([128, 8, 64]), ALU.mult), reads=[Sst, eal_], writes=[Sst])
            S.op('dve', lambda e: e.tensor_tensor(Sst[:], Sst[:], pf(1), ALU.add), reads=[Sst, PB[1]], writes=[Sst])
            S.op('pool', lambda e: e.tensor_copy(Sbf[:], Sst[:]), reads=[Sst], writes=[Sbf])
            S.mark(f'l0_{t}_vg')
            for half in range(2):
                proj_tok(w_in0, C_V + half * 512, 512, 2 + half)
            S.op('act', lambda e: e.activation(P_v[:], pf(2, 2), AF.Copy), reads=[PB[2], PB[3]], writes=[P_v])
            for half in range(2):
                proj_tok(w_in0, C_G + half * 512, 512, 2 + half)
            S.op('act', lambda e: e.activation(P_zg[:], pf(2, 2), AF.Silu), reads=[PB[2], PB[3]], writes=[P_zg])
            S.mark(f'l0_{t}_glamain')
            kl_tok = P_hb[:, 0:512]
            for ch in range(4):
                S.op('pe', lambda e, ch=ch: e.transpose(pb16(0)[:, ch * 128:(ch + 1) * 128], P_kl[:, ch, :], ident_b), reads=[P_kl, cstb], writes=[PB[0]])
            S.op('dve', lambda e: e.tensor_copy(kl_tok, pb16(0)[:, 0:512]), reads=[PB[0]], writes=[P_hb])
            for h in range(8):
                ch, hh = h // 2, h % 2
                bank = 4 + hh
                mm(pf(bank)[:, ch * 128:(ch + 1) * 128], P_kd[hh * 64:(hh + 1) * 64, ch, :], P_qd[hh * 64:(hh + 1) * 64, ch, :],
                   ch == 0, True, [P_kd, P_qd], [PB[bank]], skip_group_check=True)
            scm4 = P_scm[:].rearrange("p (c two) i -> p two c i", two=2)
            for hh in range(2):
                S.op('dve', lambda e, hh=hh: e.tensor_tensor(scm4[:, hh, :, :], pf(4 + hh).rearrange("p (c i) -> p c i", c=4),
                                                             mbd_b.unsqueeze(1).to_broadcast([128, 4, 128]), ALU.mult), reads=[PB[4 + hh], cstb], writes=[P_scm])

            def o_ap(h, p0, p1):
                return pf(6 + h % 2)[p0:p1, (h // 2) * 128:(h // 2 + 1) * 128]
            for h in range(8):
                mm(o_ap(h, 0, 128), P_scm[:, h, :], P_v[:, h * 128:(h + 1) * 128], h < 2, False, [P_scm, P_v], [PB[6 + h % 2]], skip_group_check=True)
            for c in range(2):
                gsb = GSb0 if c == 0 else GSb1
                for h in range(8):
                    ch, hh = h // 2, h % 2
                    mm(o_ap(h, c * 64, (c + 1) * 64), P_qg[hh * 64:(hh + 1) * 64, ch, c * 64:(c + 1) * 64], gsb[hh * 64:(hh + 1) * 64, ch, :],
                       False, True, [P_qg, gsb], [PB[6 + h % 2]], skip_group_check=True)
                for h in range(8):
                    ch, hh = h // 2, h % 2
                    mm(pf(1 + c)[hh * 64:(hh + 1) * 64, ch * 128:(ch + 1) * 128], kl_tok[c * 64:(c + 1) * 64, h * 64:(h + 1) * 64],
                       P_v[c * 64:(c + 1) * 64, h * 128:(h + 1) * 128], h < 2, True, [P_hb, P_v], [PB[1 + c]], skip_group_check=True)
                for ch in range(4):
                    S.op('dve', lambda e, ch=ch, c=c: e.scalar_tensor_tensor(GS[:, ch, :], GS[:, ch, :], edec_.ap[:, ch * 2 + c:ch * 2 + c + 1],
                                                                             pf(1 + c)[:, ch * 128:(ch + 1) * 128], ALU.mult, ALU.add),
                         reads=[GS, edec_, PB[1 + c]], writes=[GS])
                nxt = GSb1 if c == 0 else GSb0
                S.op('pool', lambda e, nxt=nxt: e.tensor_copy(nxt[:], GS[:]), reads=[GS], writes=[nxt])
            F1h = P_F1[:].rearrange("p (c two v) -> p two c v", two=2, v=128)
            F2h = P_F2[:].rearrange("p (c two v) -> p two c v", two=2, v=128)
            for hh in range(2):
                src = pf(6 + hh).rearrange("p (c v) -> p c v", c=4)
                S.op('act', lambda e, hh=hh, src=src: e.activation(F1h[:, hh, :, :], src, AF.Copy), reads=[PB[6 + hh]], writes=[P_F1])
                S.op('act', lambda e, hh=hh, src=src: e.activation(F2h[:, hh, :, :], src, AF.Square), reads=[PB[6 + hh]], writes=[P_F2])
            S.op('dve', lambda e: e.tensor_reduce(sso_.ap, P_F2[:].rearrange("p (h v) -> p h v", h=8), AX.X, ALU.add), reads=[P_F2], writes=[sso_])
            S.op('dve', lambda e: e.tensor_scalar(rso_.ap, sso_.ap, 1.0 / 128, EPS, ALU.mult, ALU.add), reads=[sso_], writes=[rso_])
            S.op('act', lambda e: e.activation(rso_.ap, rso_.ap, AF.Ln), reads=[rso_], writes=[rso_])
            S.op('act', lambda e: e.activation(rso_.ap, rso_.ap, AF.Exp, scale=-0.5), reads=[rso_], writes=[rso_])
            S.op('pool', lambda e: e.tensor_tensor(P_F1[:], P_F1[:], P_zg[:], ALU.mult), reads=[P_F1, P_zg], writes=[P_F1])
            S.op('dve', lambda e: e.tensor_tensor(P_mix[:, D:2 * D].rearrange("p (h v) -> p h v", h=8), P_F1[:].rearrange("p (h v) -> p h v", h=8),
                                                  rso_.ap.unsqueeze(2).to_broadcast([128, 8, 128]), ALU.mult), reads=[P_F1, rso_], writes=[P_mix])
            S.mark(f'l0_{t}_outproj')
            mixT = P_F3[:].bitcast(BF16).rearrange("p (k t) -> p k t", k=16)
            for half in range(2):
                for kk in range(8):
                    k = half * 8 + kk
                    S.op('pe', lambda e, k=k, kk=kk: e.transpose(pb16(0)[:, kk * 128:(kk + 1) * 128], P_mix[:, k * 128:(k + 1) * 128], ident_b),
                         reads=[P_mix, cstb], writes=[PB[0]])
                dst = mixT[:, half * 8:(half + 1) * 8, :].rearrange("p k t -> p (k t)")
                if half == 0:
                    S.op('dve', lambda e, dst=dst: e.tensor_copy(dst, pb16(0)), reads=[PB[0]], writes=[P_F3])
                else:
                    S.op('act', lambda e, dst=dst: e.activation(dst, pb16(0), AF.Copy), reads=[PB[0]], writes=[P_F3])
            for n in range(2):
                for k in range(16):
                    mm(pf(4 + n), mixT[:, k, :], w_out0[:, k, n * 512:(n + 1) * 512], k == 0, k == 15, [P_F3, w_out0], [PB[4 + n]])
            S.op('dve', lambda e: e.tensor_tensor(P_x[:], P_x[:], pf(4, 2), ALU.add), reads=[P_x, PB[4], PB[5]], writes=[P_x])
            dst = out_d if stage == "l0" else x1_d
            return S.dma('sp', dst[t * 128:(t + 1) * 128, :], P_x[:], reads=[P_x], writes=[X1R[t]])

        last = None
        for t in range(ntiles):
            last = layer0_tile(t)
        if stage != "l0":
            last = None
            wflat = w_in0[:].rearrange("p k n -> p (k n)")
            w_in1 = T.view(wflat[:, 0:32768].rearrange("p (k n) -> p k n", k=8), w_in0.r)
            w1v = w_in1_d.rearrange("(k p) n -> p k n", p=128)
            for c0 in range(0, 4096, 512):
                S.dma('pool', w_in1[:, :, c0:c0 + 512], w1v[:, :, c0:c0 + 512], writes=[w_in1])
            oflat = w_out0[:].rearrange("p k n -> p (k n)")
            w_out1 = T.view(oflat[:, 0:8192].rearrange("p (k n) -> p k n", k=8), w_out0.r)
            wo1v = w_out1_d.rearrange("(k p) n -> p k n", p=128)
            for k0 in range(0, 8, 4):
                S.dma('pool', w_out1[:, k0:k0 + 4, :], wo1v[:, k0:k0 + 4, :], writes=[w_out1])
            S.barrier()
            S.mark('l1_start')
            aoff[0] = 0
            VB_hi = T.view(wflat[:, 32768:32768 + 8320].rearrange("p (t h e) -> p t h e", t=8, h=16), Res("VB_hi"))
            KT_hi = T.view(oflat[:, 8192:16384].rearrange("p (c t) -> p c t", c=4), Res("KT_hi"))
            KT_lo = mk("KT_lo", [128, 4, L], BF16)
            VB_lo = mk("VB_lo", [128, 8, 16, 65], BF16)
            ksum2 = mk("ksum2", [128, 16, 8], F32)
            kmean = mk("kmean", [128, 8, 8], F32)
            fn_bc = mk("fn_bc", [128, D], F32)
            norm1 = mk("norm1", [128, 8], F32)
            X1t = mk("P_x1", [128, D], F32)
            HB1 = mk("P_hb1", [128, D], BF16)
            HT1 = mk("P_hT1", [128, 8, 128], BF16)
            F11 = mk("P_F11", [128, D], F32)
            ZG1 = mk("P_zg1", [128, D], F32)
            qT = mk("qT", [128, 8, 128], BF16)
            qTf = mk("qTf", [128, 8, 128], F32)
            PT = [mk("PT0", [128, 2, 512], BF16), mk("PT1", [128, 2, 512], BF16)]
            acc = mk("acc", [128, 16, 65], F32)
            accR = [Res(f"acc{c}") for c in range(8)]
            gm = mk("gm", [128, 16, 8], F32)
            top8 = mk("top8", [128, 16, 8], F32)
            sel = mk("sel", [128, 16, 8], F32)
            rden = mk("rden", [128, 16], F32)
            mixo = mk("mixo", [128, D], BF16)
            mixoT = mk("mixoT", [128, 8, 128], BF16)
            sm1 = mk("sm1", [128, 8], F32)
            ss1 = T.view(sm1[:, 0:1], Res("ss1")); rs1 = T.view(sm1[:, 1:2], Res("rs1"))

            class V1:
                pass
            ssv = V1(); ssv.ap = ss1.t; ssv.r = ss1.r
            rsv = V1(); rsv.ap = rs1.t; rsv.r = rs1.r

            def norm_transpose1(bank):
                rms_rstd(X1t[:], X1t, ssv, rsv, D, F11[:], F11)
                S.op('dve', lambda e: e.tensor_scalar(HB1[:], X1t[:], rsv.ap, None, ALU.mult), reads=[X1t, rsv], writes=[HB1])
                for k in range(8):
                    S.op('pe', lambda e, k=k: e.transpose(pb16(bank)[:, k * 128:(k + 1) * 128], HB1[:, k * 128:(k + 1) * 128], ident_b),
                         reads=[HB1, cstb], writes=[PB[bank]])
                S.op('act', lambda e: e.activation(HT1[:].rearrange("p k t -> p (k t)"), pb16(bank), AF.Copy), reads=[PB[bank]], writes=[HT1])

            def proj_tok1(w, c0, n, bank):
                for k in range(8):
                    mm(pf(bank)[:, 0:n], HT1[:, k, :], w[:, k, c0:c0 + n], k == 0, k == 7, [HT1, w], [PB[bank]])

            def proj_fm1(w, c0, m, bank, slot):
                for k in range(8):
                    mm(pf(bank)[0:m, slot * 128:(slot + 1) * 128], w[:, k, c0:c0 + m], HT1[:, k, :], (k == 0 and slot == 0), k == 7,
                       [HT1, w], [PB[bank]], skip_group_check=True)

            S.dma('sp', norm1[:], norm1_d[:, :], writes=[norm1])
            S.dma('sp', fn_bc[:], fnorm_d.partition_broadcast(128), writes=[fn_bc])
            for k in range(8):
                eng = 'dve' if k % 2 == 0 else 'pool'
                S.op(eng, lambda e, k=k: e.tensor_scalar(w_in1[:, k, :], w_in1[:, k, :], norm1[:, k:k + 1], None, ALU.mult),
                     reads=[w_in1, norm1], writes=[w_in1])
            S.op('dve', lambda e: e.memset(VB_lo[:, :, :, 64:65], 1.0), writes=[VB_lo])
            S.op('dve', lambda e: e.memset(VB_hi[:, :, :, 64:65], 1.0), writes=[VB_hi])

            def KT(c):
                return (KT_lo, c) if c < 4 else (KT_hi, c - 4)

            def VB(t):
                return (VB_lo, t) if t < 8 else (VB_hi, t - 8)

            for t in range(ntiles):
                S.dma('sp', X1t[:], x1_d[t * 128:(t + 1) * 128, :], reads=[X1R[t]], writes=[X1t])
                norm_transpose1(0)
                for c in range(8): proj_fm1(w_in1, 1024 + c * 128, 128, 1 + c // 4, c % 4)
                for half in range(2):
                    kt_, _ = KT(half * 4)
                    src = pf(1 + half).rearrange("p (c t) -> p c t", c=4)
                    S.op('act', lambda e, kt_=kt_, src=src, t=t: e.activation(kt_[:, :, t * 128:(t + 1) * 128], src, AF.Copy), reads=[PB[1 + half]], writes=[kt_])
                    S.op('dve', lambda e, half=half, src=src, t=t: e.tensor_reduce(ksum2[:, t, half * 4:(half + 1) * 4], src, AX.X, ALU.add),
                         reads=[PB[1 + half]], writes=[ksum2])
                for half in range(2):
                    proj_tok1(w_in1, 2048 + half * 512, 512, 3 + half)
                vt, ti = VB(t)
                S.op('act', lambda e, vt=vt, ti=ti: e.activation(vt[:, ti, :, 0:64], pf(3, 2).rearrange("p (h d) -> p h d", h=16), AF.Copy),
                     reads=[PB[3], PB[4]], writes=[vt])
            k2 = ksum2[:].rearrange("p (b two) c -> p c b two", two=2)
            S.op('dve', lambda e: e.tensor_tensor(kmean[:], k2[:, :, :, 0], k2[:, :, :, 1], ALU.add), reads=[ksum2], writes=[kmean])
            S.op('dve', lambda e: e.tensor_scalar(kmean[:], kmean[:], 1.0 / 256, None, ALU.mult), reads=[kmean], writes=[kmean])
            S.mark('l1_phaseB')

            STB = [(3, 4), (1, 2)]
            gcount = [0]
            for t in range(ntiles):
                b = t // 2
                causal_only = b < 4
                S.dma('sp', X1t[:], x1_d[t * 128:(t + 1) * 128, :], reads=[X1R[t]], writes=[X1t])
                norm_transpose1(0)
                for c in range(8): proj_fm1(w_in1, c * 128, 128, 1 + c // 4, c % 4)
                for half in range(2):
                    dstb = qT[:, half * 4:(half + 1) * 4, :].rearrange("p c t -> p (c t)")
                    S.op('act', lambda e, half=half, dstb=dstb: e.mul(dstb, pf(1 + half), 0.125), reads=[PB[1 + half]], writes=[qT])
                    if not causal_only:
                        dstf = qTf[:, half * 4:(half + 1) * 4, :].rearrange("p c t -> p (c t)")
                        S.op('dve', lambda e, half=half, dstf=dstf: e.tensor_scalar(dstf, pf(1 + half), 0.125, None, ALU.mult), reads=[PB[1 + half]], writes=[qTf])
                for half in range(2):
                    proj_tok1(w_in1, 3072 + half * 512, 512, 3 + half)
                S.op('act', lambda e: e.activation(ZG1[:], pf(3, 2), AF.Silu), reads=[PB[3], PB[4]], writes=[ZG1])
                if not causal_only:
                    for c in range(8):
                        for hh in range(2):
                            mm(pf(1 + hh)[:, c * 8:(c + 1) * 8], qTf[hh * 64:(hh + 1) * 64, c, :], kmean[hh * 64:(hh + 1) * 64, c, :],
                               c == 0, True, [qTf, kmean], [PB[1 + hh]], skip_group_check=True)
                    gm4 = gm[:].rearrange("p (c two) n -> p c two n", two=2)
                    S.op('dve', lambda e: e.memset(gm[:], NEG), writes=[gm])
                    for hh in range(2):
                        S.op('dve', lambda e, hh=hh, b=b: e.tensor_copy(gm4[:, :, hh, 0:b], pf(1 + hh)[:, 0:64].rearrange("p (c n) -> p c n", c=8)[:, :, 0:b]),
                             reads=[PB[1 + hh]], writes=[gm])
                    for h in range(16):
                        S.op('dve', lambda e, h=h: e.max(top8[:, h, :], gm[:, h, :]), reads=[gm], writes=[top8])
                    S.op('dve', lambda e: e.tensor_tensor(sel[:], gm[:], top8[:, :, 2:3].to_broadcast([128, 16, 8]), ALU.is_ge), reads=[gm, top8], writes=[sel])
                for c in range(8):
                    ktens, ci = KT(c)
                    fb = {5: True, 6: True, 7: True}
                    for g0 in range(0, t + 1, 4):
                        grp = list(range(g0, min(g0 + 4, t + 1))); ns = len(grp)
                        sb = STB[gcount[0] % 2]; pt = PT[gcount[0] % 2]; gcount[0] += 1
                        for hh in range(2):
                            for s_, kt in enumerate(grp):
                                o = pf(sb[hh])[:, s_ * 128:(s_ + 1) * 128]
                                mm(o, ktens[hh * 64:(hh + 1) * 64, ci, kt * 128:(kt + 1) * 128], qT[hh * 64:(hh + 1) * 64, c, :], s_ == 0, kt != t,
                                   [ktens, qT], [PB[sb[hh]]], skip_group_check=True)
                                if kt == t:
                                    mm(o, ident_b, mneg_b, False, True, [cstb], [PB[sb[hh]]], skip_group_check=True)
                        for hh in range(2):
                            S.op('act', lambda e, hh=hh, pt=pt, sb=sb, ns=ns: e.activation(pt[:, hh, 0:ns * 128], pf(sb[hh])[:, 0:ns * 128], AF.Exp),
                                 reads=[PB[sb[hh]]], writes=[pt])
                        for hh in range(2):
                            h = 2 * c + hh
                            for s_, kt in enumerate(grp):
                                n = kt // 2
                                vt, ti = VB(kt)
                                if causal_only or n == b:
                                    bank = 7; o = pf(7)[:, hh * 65:(hh + 1) * 65]
                                else:
                                    bank = 5 + hh; o = pf(bank)[:, n * 65:(n + 1) * 65]
                                mm(o, pt[:, hh, s_ * 128:(s_ + 1) * 128], vt[:, ti, h, :], fb[bank], True, [pt, vt], [PB[bank]], skip_group_check=True)
                                fb[bank] = False
                    for hh in range(2):
                        h = 2 * c + hh
                        S.op('dve', lambda e, h=h, hh=hh: e.tensor_copy(acc[:, h, :], pf(7)[:, hh * 65:(hh + 1) * 65]), reads=[PB[7]], writes=[accR[c]])
                        if not causal_only:
                            for n in range(b):
                                S.op('dve', lambda e, h=h, hh=hh, n=n: e.scalar_tensor_tensor(acc[:, h, :], pf(5 + hh)[:, n * 65:(n + 1) * 65], sel[:, h, n:n + 1],
                                                                                                acc[:, h, :], ALU.mult, ALU.add),
                                     reads=[PB[5 + hh], sel, accR[c]], writes=[accR[c]])
                S.op('dve', lambda e: e.reciprocal(rden[:], acc[:, :, 64]), reads=accR, writes=[rden])
                S.op('dve', lambda e: e.tensor_tensor(F11[:].rearrange("p (h d) -> p h d", h=16), acc[:, :, 0:64],
                                                      rden[:].unsqueeze(2).to_broadcast([128, 16, 64]), ALU.mult), reads=accR + [rden], writes=[F11])
                S.op('pool', lambda e: e.tensor_tensor(mixo[:], F11[:], ZG1[:], ALU.mult), reads=[F11, ZG1], writes=[mixo])
                for k in range(8):
                    S.op('pe', lambda e, k=k: e.transpose(pb16(0)[:, k * 128:(k + 1) * 128], mixo[:, k * 128:(k + 1) * 128], ident_b),
                         reads=[mixo, cstb], writes=[PB[0]])
                S.op('act', lambda e: e.activation(mixoT[:].rearrange("p k t -> p (k t)"), pb16(0), AF.Copy), reads=[PB[0]], writes=[mixoT])
                for n in range(2):
                    for k in range(8):
                        mm(pf(5 + n), mixoT[:, k, :], w_out1[:, k, n * 512:(n + 1) * 512], k == 0, k == 7, [mixoT, w_out1], [PB[5 + n]])
                S.op('dve', lambda e: e.tensor_tensor(X1t[:], X1t[:], pf(5, 2), ALU.add), reads=[X1t, PB[5], PB[6]], writes=[X1t])
                rms_rstd(X1t[:], X1t, ssv, rsv, D, F11[:], F11)
                S.op('dve', lambda e: e.scalar_tensor_tensor(F11[:], X1t[:], rsv.ap, fn_bc[:], ALU.mult, ALU.mult), reads=[X1t, rsv, fn_bc], writes=[F11])
                last = S.dma('sp', out_d[t * 128:(t + 1) * 128, :], F11[:], reads=[F11])
        S.barrier()
        S.emit()
        build.marks = dict(S.marks); build.nops = S.nops
    return nc


_CONST = None


def _consts():
    global _CONST
    if _CONST is None:
        i = np.arange(128)
        ident = np.eye(128, dtype=np.float32)
        tri = (i[:, None] <= i[None, :]).astype(np.float32)
        ones = np.ones((128, 128), np.float32)
        mneg = np.where(i[None, :] < i[:, None], NEG, 0.0).astype(np.float32)
        mbd = ((i[:, None] // 64 == i[None, :] // 64) & (i[None, :] >= i[:, None])).astype(np.float32)
        cst = np.concatenate([ident, tri, ones, mneg, mbd], axis=1)
        rmask = np.ones((128, 512), np.float32); rmask[:, ::64] = 0.0
        _CONST = (np.ascontiguousarray(cst), rmask)
    return _CONST


def _fm(v, k):
    return np.ascontiguousarray(np.asarray(v, np.float32).reshape(k, 128).T)


def make_in_maps(inp):
    cst, rmask = _consts()
    shared = {
        "w_in0": np.ascontiguousarray(inp["even_w_in"][0]),
        "w_out0": np.ascontiguousarray(inp["even_w_out"][0]),
        "w_in1": np.ascontiguousarray(inp["odd_w_in"][0]),
        "w_out1": np.ascontiguousarray(inp["odd_w_out"][0]),
        "norm0": _fm(inp["even_norm"][0], 8),
        "wo0s": np.ascontiguousarray(np.concatenate([_fm(inp["even_ssd_norm"][0], 8), np.tile(inp["even_gla_norm"][0][:, None], (1, 8))], axis=1).astype(np.float32)),
        "norm1": _fm(inp["odd_norm"][0], 8),
        "fnorm": np.ascontiguousarray(inp["final_norm"].reshape(1, D)),
        "convw": np.ascontiguousarray(inp["even_conv_w"][0].reshape(4, 10, 128).transpose(2, 1, 0).reshape(128, 40)),
        "convb": np.ascontiguousarray(inp["even_conv_b"][0].reshape(1, 1280)),
        "hvec": np.ascontiguousarray(np.concatenate([inp["even_a_log"][0], inp["even_dt_bias"][0], inp["even_d_skip"][0]]).reshape(1, 48)),
        "gw2": np.ascontiguousarray(inp["even_gate_w2"][0]),
        "gb": _fm(inp["even_gate_b"][0], 4),
        "cst": cst, "rmask": rmask,
    }
    maps = []
    for b in range(8):
        m = dict(shared)
        m["x"] = np.ascontiguousarray(inp["x"][b])
        maps.append(m)
    return maps


_NC = {}


def kernel(**inputs):
    inp = {k: np.asarray(v) for k, v in inputs.items()}
    stage = "full"
    if stage not in _NC:
        _NC[stage] = build(stage)
    res = run_bass_kernel_spmd(_NC[stage], make_in_maps(inp), core_ids=list(range(8)))
    return np.stack([r["out"] for r in res.results], axis=0).astype(np.float32)
```
